# Optimizing a Trainium2 kernel written in Bass

```python
import jax, jax.numpy as jnp
from jax import lax
import numpy as np

D_MODEL = 1024
BATCH = 8
SEQ = 4096
DEPTH = 1

CHUNK = 64
D_MIX = D_MODEL
SWA_HEAD_DIM = 64
SWA_HEADS = (D_MIX // 2) // SWA_HEAD_DIM
SWA_KV_HEADS = 2
SWA_WIDTH = SWA_HEADS * SWA_HEAD_DIM
SWA_KV_WIDTH = SWA_KV_HEADS * SWA_HEAD_DIM
WINDOW = 128
WINDOW_CHUNKS = WINDOW // CHUNK
HGRN_HEAD_DIM = 128
HGRN_WIDTH = D_MIX - SWA_WIDTH
HGRN_HEADS = HGRN_WIDTH // HGRN_HEAD_DIM
IN_SIZES = (SWA_WIDTH, SWA_KV_WIDTH, SWA_KV_WIDTH,
            HGRN_WIDTH, HGRN_WIDTH, HGRN_WIDTH, HGRN_WIDTH)
D_IN = sum(IN_SIZES)
IN_SPLITS = [int(v) for v in np.cumsum(IN_SIZES)[:-1]]
MEM_LEN = 256
XATTN_HEADS = 4
XATTN_HEAD_DIM = D_MODEL // XATTN_HEADS
D_FF = ((8 * D_MODEL // 3 + 255) // 256) * 256
RMS_EPS = 1e-6
NEG_INF = -1e30

kernel_name = "hymba_swa_sink_hgrn2_xattn_layer"


def rms_norm(x, g):
    xf = x.astype(jnp.float32)
    y = xf * lax.rsqrt(jnp.mean(xf * xf, axis=-1, keepdims=True) + RMS_EPS)
    return (y * g.astype(jnp.float32)).astype(x.dtype)


def swa_with_sinks(q, k, v, sinks):
    B, T, Hq, Dh = q.shape
    Hkv = k.shape[2]
    G = Hq // Hkv
    NC = T // CHUNK
    WC = WINDOW_CHUNKS
    L = (WC + 1) * CHUNK
    qc = q.reshape(B, NC, CHUNK, Hkv, G, Dh)
    pad = ((0, 0), (WC * CHUNK, 0), (0, 0), (0, 0))
    kc = jnp.pad(k, pad).reshape(B, NC + WC, CHUNK, Hkv, Dh)
    vc = jnp.pad(v, pad).reshape(B, NC + WC, CHUNK, Hkv, Dh)
    kband = jnp.concatenate([kc[:, j:j + NC] for j in range(WC + 1)], axis=2)
    vband = jnp.concatenate([vc[:, j:j + NC] for j in range(WC + 1)], axis=2)
    band_chunk = jnp.arange(NC)[:, None] - WC + jnp.arange(WC + 1)[None, :]
    valid = jnp.repeat(band_chunk >= 0, CHUNK, axis=1)
    s = jnp.einsum('bnqhgd,bnkhd->bnhgqk', qc, kband).astype(jnp.float32) * (Dh ** -0.5)
    s = jnp.where(valid[None, :, None, None, None, :], s, NEG_INF)
    sink = jnp.broadcast_to(sinks.astype(jnp.float32).reshape(1, 1, Hkv, G, 1, 1),
                            (B, NC, Hkv, G, CHUNK, 1))
    p = jax.nn.softmax(jnp.concatenate([s, sink], axis=-1), axis=-1)[..., :L]
    o = jnp.einsum('bnhgqk,bnkhd->bnqhgd', p.astype(v.dtype), vband)
    return o.reshape(B, T, Hq * Dh)


def hgrn2(q, f_logit, i, g, lb, onorm_g):
    B, T, H, Dk = q.shape
    Dv = i.shape[-1]
    NC = T // CHUNK
    f32 = jnp.float32
    qf = jax.nn.silu(q.astype(f32)) * (Dk ** -0.5)
    lbf = lb.astype(f32)
    f = lbf + (1.0 - lbf) * jax.nn.sigmoid(f_logit.astype(f32))
    kf = 1.0 - f
    logf = jnp.log(f)

    def chunks(a):
        return a.reshape(B, NC, CHUNK, H, a.shape[-1]).transpose(0, 3, 1, 2, 4)

    qc, kc, vc, lc = chunks(qf), chunks(kf), chunks(i.astype(f32)), chunks(logf)
    b = jnp.cumsum(lc, axis=3)
    b_mid = b[:, :, :, CHUNK // 2 - 1:CHUNK // 2]
    b_last = b[:, :, :, CHUNK - 1:CHUNK]
    A = jnp.einsum('bhnqd,bhnkd->bhnqk', qc * jnp.exp(b - b_mid), kc * jnp.exp(b_mid - b))
    causal = jnp.tril(jnp.ones((CHUNK, CHUNK), dtype=bool))
    A = jnp.where(causal, A, 0.0)
    o_intra = jnp.einsum('bhnqk,bhnkv->bhnqv', A, vc)
    kv = jnp.einsum('bhnkd,bhnkv->bhndv', kc * jnp.exp(b_last - b), vc)
    decay = jnp.exp(b_last[:, :, :, 0, :])

    def step(S, inp):
        d, u = inp
        return d[..., None] * S + u, S

    S0 = jnp.zeros((B, H, Dk, Dv), f32)
    _, S_prev = lax.scan(step, S0, (jnp.moveaxis(decay, 2, 0), jnp.moveaxis(kv, 2, 0)))
    S_prev = jnp.moveaxis(S_prev, 0, 2)
    o_inter = jnp.einsum('bhnqd,bhndv->bhnqv', qc * jnp.exp(b), S_prev)
    o = (o_intra + o_inter).transpose(0, 2, 3, 1, 4).reshape(B, T, H, Dv)
    o = rms_norm(o, onorm_g) * jax.nn.silu(g.astype(f32))
    return o.reshape(B, T, H * Dv).astype(q.dtype)


def cross_attention(u, m, wq, wk, wv, wo):
    B, T, _ = u.shape
    M = m.shape[1]
    q = (u @ wq).reshape(B, T, XATTN_HEADS, XATTN_HEAD_DIM)
    k = (m @ wk).reshape(B, M, XATTN_HEADS, XATTN_HEAD_DIM)
    v = (m @ wv).reshape(B, M, XATTN_HEADS, XATTN_HEAD_DIM)
    s = jnp.einsum('bthd,bmhd->bhtm', q, k).astype(jnp.float32) * (XATTN_HEAD_DIM ** -0.5)
    p = jax.nn.softmax(s, axis=-1).astype(v.dtype)
    o = jnp.einsum('bhtm,bmhd->bthd', p, v).reshape(B, T, D_MODEL)
    return o @ wo


def setup_inputs(seed: int = 0) -> dict:
    key = jax.random.key(seed)
    ks = jax.random.split(key, 24)
    f32 = jnp.float32

    def nrm(k, shape, scale):
        return jax.random.normal(k, shape, f32) * scale

    def gain(k, shape):
        return 1.0 + 0.05 * jax.random.normal(k, shape, f32)

    return {
        "x": nrm(ks[0], (BATCH, SEQ, D_MODEL), 1.0),
        "mem": nrm(ks[1], (BATCH, MEM_LEN, D_MODEL), 1.0),
        "w_in": nrm(ks[2], (DEPTH, D_MODEL, D_IN), D_MODEL ** -0.5),
        "sinks": nrm(ks[3], (DEPTH, SWA_HEADS), 0.5),
        "hgrn_lb": nrm(ks[4], (DEPTH + 1, HGRN_WIDTH), 0.1),
        "hgrn_onorm": gain(ks[5], (DEPTH, HGRN_HEAD_DIM)),
        "w_out": nrm(ks[6], (DEPTH, D_MIX, D_MODEL), D_MIX ** -0.5),
        "g_mix_pre": gain(ks[7], (DEPTH, D_MODEL)),
        "g_mix_post": gain(ks[8], (DEPTH, D_MODEL)),
        "g_mem": gain(ks[9], (DEPTH, D_MODEL)),
        "g_x_pre": gain(ks[10], (DEPTH, D_MODEL)),
        "g_x_post": gain(ks[11], (DEPTH, D_MODEL)),
        "wq_x": nrm(ks[12], (DEPTH, D_MODEL, D_MODEL), D_MODEL ** -0.5),
        "wk_x": nrm(ks[13], (DEPTH, D_MODEL, D_MODEL), D_MODEL ** -0.5),
        "wv_x": nrm(ks[14], (DEPTH, D_MODEL, D_MODEL), D_MODEL ** -0.5),
        "wo_x": nrm(ks[15], (DEPTH, D_MODEL, D_MODEL), D_MODEL ** -0.5),
        "g_ffn_pre": gain(ks[16], (DEPTH, D_MODEL)),
        "g_ffn_post": gain(ks[17], (DEPTH, D_MODEL)),
        "w_gate": nrm(ks[18], (DEPTH, D_MODEL, D_FF), D_MODEL ** -0.5),
        "w_up": nrm(ks[19], (DEPTH, D_MODEL, D_FF), D_MODEL ** -0.5),
        "w_down": nrm(ks[20], (DEPTH, D_FF, D_MODEL), D_FF ** -0.5),
    }


def reference(x, mem, w_in, sinks, hgrn_lb, hgrn_onorm, w_out, g_mix_pre, g_mix_post,
              g_mem, g_x_pre, g_x_post, wq_x, wk_x, wv_x, wo_x, g_ffn_pre, g_ffn_post,
              w_gate, w_up, w_down):
    B, T, _ = x.shape
    lb_all = jnp.cumsum(jax.nn.softmax(hgrn_lb.astype(jnp.float32), axis=0), axis=0)
    h = x
    for l in range(DEPTH):
        u = rms_norm(h, g_mix_pre[l])
        z = u @ w_in[l]
        qa, ka, va, qh, fh, ih, gh = jnp.split(z, IN_SPLITS, axis=-1)
        ya = swa_with_sinks(qa.reshape(B, T, SWA_HEADS, SWA_HEAD_DIM),
                            ka.reshape(B, T, SWA_KV_HEADS, SWA_HEAD_DIM),
                            va.reshape(B, T, SWA_KV_HEADS, SWA_HEAD_DIM),
                            sinks[l])
        hv = HGRN_WIDTH // HGRN_HEADS
        yh = hgrn2(qh.reshape(B, T, HGRN_HEADS, HGRN_HEAD_DIM),
                   fh.reshape(B, T, HGRN_HEADS, HGRN_HEAD_DIM),
                   ih.reshape(B, T, HGRN_HEADS, hv),
                   gh.reshape(B, T, HGRN_HEADS, hv),
                   lb_all[l].reshape(HGRN_HEADS, HGRN_HEAD_DIM),
                   hgrn_onorm[l])
        y = jnp.concatenate([ya, yh.astype(ya.dtype)], axis=-1) @ w_out[l]
        h = h + rms_norm(y, g_mix_post[l])
        u = rms_norm(h, g_x_pre[l])
        m = rms_norm(mem, g_mem[l])
        y = cross_attention(u, m, wq_x[l], wk_x[l], wv_x[l], wo_x[l])
        h = h + rms_norm(y, g_x_post[l])
        u = rms_norm(h, g_ffn_pre[l])
        y = (jax.nn.silu(u @ w_gate[l]) * (u @ w_up[l])) @ w_down[l]
        h = h + rms_norm(y, g_ffn_post[l])
    return h
```

```python
import numpy as np
from contextlib import ExitStack

import concourse.bass as bass
import concourse.mybir as mybir
from concourse.bass_utils import run_bass_kernel_spmd

F32 = mybir.dt.float32
BF16 = mybir.dt.bfloat16
AF = mybir.ActivationFunctionType
ALU = mybir.AluOpType

PE, ACT, DVE, POOL, SP = "tensor", "scalar", "vector", "gpsimd", "sync"
ENGS = (PE, ACT, DVE, POOL, SP)

D = 1024
TT = 512
MEM = 256
DFF = 2816
NIN = 2944
FM_CH = 18
VCOL = 2304
FTM = 1280
ITM = 2432
EPS = 1e-6

G_MIX_PRE, G_MIX_POST, G_X_PRE, G_X_POST, G_FFN_PRE, G_FFN_POST, G_MEM = 0, 8, 16, 24, 32, 40, 48
ONG, LB0, LB1, SNK = 56, 57, 61, 65
NPP = 73


class Prog:
    def __init__(self, sems, dma_sems):
        self.sem = sems
        self.dma_sems = list(dma_sems)
        self.stream_sem = {}
        self.stream_cnt = {}
        self.cnt = {e: 0 for e in ENGS}
        self.lists = {e: [] for e in ENGS}
        self.waited = {e: {} for e in ENGS}
        self.last_w = {}
        self.readers = {}
        self.sem_by_id = {}
        for e in ENGS:
            self.sem_by_id[id(sems[e])] = sems[e]
        self.stage = ""
        self.labels = {e: [] for e in ENGS}

    def _deps(self, reads, writes):
        deps = []
        for k in reads:
            if k in self.last_w:
                deps.append(self.last_w[k])
        for k in writes:
            if k in self.last_w:
                deps.append(self.last_w[k])
            deps.extend(self.readers.get(k, ()))
        return deps

    def _emit_waits(self, eng, deps):
        best = {}
        for (sid, val, peng) in deps:
            if peng == eng and eng in (PE, SP):
                continue
            if self.waited[eng].get(sid, 0) >= val:
                continue
            if best.get(sid, 0) < val:
                best[sid] = val
        for sid, val in best.items():
            self.waited[eng][sid] = val
            sem = self.sem_by_id[sid]
            self.lists[eng].append(lambda e, sem=sem, val=val: e.wait_ge(sem, val))

    def _commit(self, tok, reads, writes):
        for k in reads:
            self.readers.setdefault(k, []).append(tok)
        for k in writes:
            self.last_w[k] = tok
            self.readers[k] = []

    def op(self, eng, fn, reads=(), writes=()):
        reads = list(reads)
        writes = list(writes)
        self._emit_waits(eng, self._deps(reads, writes))
        self.cnt[eng] += 1
        self.labels[eng].append(self.stage)
        sem = self.sem[eng]
        tok = (id(sem), self.cnt[eng], eng)
        self.lists[eng].append(lambda e, fn=fn, sem=sem: fn(e).then_inc(sem, 1))
        self._commit(tok, reads, writes)

    def dma(self, qeng, stream, fn, reads=(), writes=()):
        reads = list(reads)
        writes = list(writes)
        if stream not in self.stream_sem:
            self.stream_sem[stream] = self.dma_sems.pop()
            self.stream_cnt[stream] = 0
            s = self.stream_sem[stream]
            self.sem_by_id[id(s)] = s
        sem = self.stream_sem[stream]
        self._emit_waits(qeng, self._deps(reads, writes))
        self.stream_cnt[stream] += 16
        tok = (id(sem), self.stream_cnt[stream], None)
        self.lists[qeng].append(lambda e, fn=fn, sem=sem: fn(e).then_inc(sem, 16))
        self._commit(tok, reads, writes)

    def barrier(self):
        toks = [(id(self.sem[e]), self.cnt[e], e) for e in ENGS if self.cnt[e] > 0 and e != SP]
        toks += [(id(s), self.stream_cnt[n], None) for n, s in self.stream_sem.items()]
        for e in ENGS:
            self._emit_waits(e, [t for t in toks if t[2] != e or e not in (PE, SP)])
        self.last_w = {}
        self.readers = {}

    def wait_keys(self, eng, keys):
        deps = [self.last_w[k] for k in keys if k in self.last_w]
        self._emit_waits(eng, deps)

    def replay(self, block):
        lists = self.lists

        @block.tensor
        def _(e):
            for f in lists[PE]:
                f(e)

        @block.scalar
        def _(e):
            for f in lists[ACT]:
                f(e)

        @block.vector
        def _(e):
            for f in lists[DVE]:
                f(e)

        @block.gpsimd
        def _(e):
            for f in lists[POOL]:
                f(e)

        @block.sync
        def _(e):
            for f in lists[SP]:
                f(e)


def mm(out, lhsT, rhs, start, stop):
    return lambda e: e.matmul(out, lhsT, rhs, start=start, stop=stop)


def group(fns):
    def f(e):
        last = None
        for g in fns:
            last = g(e)
        return last
    return f


def act(out, in_, func, bias=None, scale=None):
    kw = {}
    if bias is not None:
        kw["bias"] = bias
    if scale is not None:
        kw["scale"] = scale
    return lambda e: e.activation(out=out, in_=in_, func=func, **kw)


def tt(out, in0, in1, op):
    return lambda e: e.tensor_tensor(out=out, in0=in0, in1=in1, op=op)


def ts(out, in0, s1, s2, op0, op1):
    return lambda e: e.tensor_scalar(out=out, in0=in0, scalar1=s1, scalar2=s2, op0=op0, op1=op1)


def stt(out, in0, scalar, in1, op0, op1):
    return lambda e: e.scalar_tensor_tensor(out=out, in0=in0, scalar=scalar, in1=in1, op0=op0, op1=op1)


def cp(out, in_):
    return lambda e: e.tensor_copy(out=out, in_=in_)


def recip(out, in_):
    return lambda e: e.reciprocal(out=out, in_=in_)


def mset(ap, v):
    return lambda e: e.memset(ap, v)


def dma(out, in_):
    return lambda e: e.dma_start(out=out, in_=in_)


class Arena:
    def __init__(self, tensor, nbytes):
        self.t = tensor
        self.nbytes = nbytes
        self.off = 0

    def reset(self, off=0):
        self.off = off

    def alloc(self, shape, dtype):
        esz = 4 if dtype == F32 else 2
        n = int(np.prod(shape[1:]))
        nb = (n * esz + 31) // 32 * 32
        assert self.off + nb <= self.nbytes, ("SBUF arena overflow", self.off, nb, self.nbytes)
        a = self.t[:, self.off // 4:(self.off + nb) // 4]
        if dtype != F32:
            a = a.bitcast(dtype)
        a = a[:, 0:n]
        self.off += nb
        if len(shape) == 3:
            a = a.rearrange("p (a b) -> p a b", a=shape[1])
        elif len(shape) == 4:
            a = a.rearrange("p (a b c) -> p a b c", a=shape[1], b=shape[2])
        return a


class _Stop(Exception):
    pass


def build(T, debug=False, stop_at=None):
    NT = T // TT

    _prog = []

    def ckpt(name):
        if _prog:
            _prog[0].stage = name
        if stop_at is not None and name == stop_at:
            raise _Stop()
    nc = bass.Bass("TRN2", target_bir_lowering=False)
    xT = nc.dram_tensor("xT", [D, T], F32, kind="ExternalInput").ap()
    memT = nc.dram_tensor("memT", [D, MEM], F32, kind="ExternalInput").ap()
    w_in = nc.dram_tensor("w_in", [D, NIN], F32, kind="ExternalInput").ap()
    w_out = nc.dram_tensor("w_out", [D, D], F32, kind="ExternalInput").ap()
    wq = nc.dram_tensor("wq", [D, D], F32, kind="ExternalInput").ap()
    wk = nc.dram_tensor("wk", [D, D], F32, kind="ExternalInput").ap()
    wv = nc.dram_tensor("wv", [D, D], F32, kind="ExternalInput").ap()
    wo = nc.dram_tensor("wo", [D, D], F32, kind="ExternalInput").ap()
    w_gate = nc.dram_tensor("w_gate", [D, DFF], F32, kind="ExternalInput").ap()
    w_up = nc.dram_tensor("w_up", [D, DFF], F32, kind="ExternalInput").ap()
    w_down = nc.dram_tensor("w_down", [DFF, D], F32, kind="ExternalInput").ap()
    pp_d = nc.dram_tensor("pp", [128, NPP], F32, kind="ExternalInput").ap()
    lbrow_d = nc.dram_tensor("lbrow", [1024], F32, kind="ExternalInput").ap()
    consts_d = nc.dram_tensor("consts", [128, 320], F32, kind="ExternalInput").ap()
    outT = nc.dram_tensor("outT", [D, T], F32, kind="ExternalOutput").ap()
    wb = {}
    for nm, shp in (("wq", [D, D]), ("wk", [D, D]), ("wv", [D, D]), ("wo", [D, D]),
                    ("w_gate", [D, DFF]), ("w_up", [D, DFF]), ("w_down", [DFF, D])):
        wb[nm] = nc.dram_tensor(nm + "_b", shp, BF16, kind="Internal").ap()
    kind_dbg = "ExternalOutput" if debug else "Internal"
    h1T = nc.dram_tensor("h1T", [D, T], F32, kind=kind_dbg).ap()
    h2T = nc.dram_tensor("h2T", [D, T], F32, kind=kind_dbg).ap()

    def dview(ap):
        return ap.rearrange("(c p) t -> p c t", p=128)

    with ExitStack() as es:
        NBYTES = 212480
        arena_t = es.enter_context(nc.sbuf_tensor("arena", [128, NBYTES // 4], F32))
        AR = Arena(arena_t, NBYTES)
        banks = [es.enter_context(nc.psum_tensor(f"bank{i}", [128, 512], F32)) for i in range(8)]
        sems = {e: es.enter_context(nc.semaphore(f"s_{e}")) for e in ENGS}
        dsems = [es.enter_context(nc.semaphore(f"d{i}")) for i in range(40)]
        block = es.enter_context(nc.Block())
        P = Prog(sems, dsems)
        _prog.append(P)

        try:
            def bk(i):
                return banks[i], f"ps{i}"

            PPt = AR.alloc([128, NPP], F32)
            CON = AR.alloc([128, 320], F32)
            TRIINC = CON[:, 0:128]
            TRIREV = CON[:, 128:256]
            MASK = CON[:, 256:320]
            ONES = AR.alloc([128, 128], BF16)
            LBP = AR.alloc([128, 12], F32)
            ES = AR.alloc([128, 8], F32)
            RSTD = AR.alloc([128, 512], F32)
            base_off = AR.off

            P.dma(SP, "c0", dma(PPt, pp_d), writes=["PP"])
            P.dma(SP, "c1", dma(CON, consts_d), writes=["CON"])
            P.op(POOL, mset(ONES, 1.0), writes=["ONES"])
            P.op(DVE, tt(LBP[:, 0:4], PPt[:, LB1:LB1 + 4], PPt[:, LB0:LB0 + 4], ALU.subtract), reads=["PP"], writes=["LBP"])
            P.op(ACT, act(LBP[:, 0:4], LBP[:, 0:4], AF.Exp), reads=["LBP"], writes=["LBP"])
            P.op(DVE, ts(LBP[:, 0:4], LBP[:, 0:4], 1.0, None, ALU.add, ALU.bypass) if False else
                 (lambda e: e.tensor_scalar_add(out=LBP[:, 0:4], in0=LBP[:, 0:4], scalar1=1.0)), reads=["LBP"], writes=["LBP"])
            P.op(DVE, recip(LBP[:, 0:4], LBP[:, 0:4]), reads=["LBP"], writes=["LBP"])
            P.op(DVE, ts(LBP[:, 4:8], LBP[:, 0:4], -1.0, 1.0, ALU.mult, ALU.add), reads=["LBP"], writes=["LBP"])
            P.op(DVE, ts(LBP[:, 8:12], LBP[:, 0:4], 1.0, -1.0, ALU.mult, ALU.add), reads=["LBP"], writes=["LBP"])
            P.op(ACT, act(ES, PPt[:, SNK:SNK + 8], AF.Exp), reads=["PP"], writes=["ES"])

            def rstd_from_sq(SQ, nfree, dim, bank_i, out_rstd, sqkeys, rkey="RSTD"):
                b, bkey = bk(bank_i)
                nch = SQ.shape[1]
                P.op(PE, group([mm(b[:, 0:nfree], ONES, SQ[:, c, :], c == 0, c == nch - 1) for c in range(nch)]),
                     reads=list(sqkeys) + ["ONES"], writes=[bkey])
                P.op(ACT, act(out_rstd, b[:, 0:nfree], AF.Ln, bias=EPS, scale=1.0 / dim), reads=[bkey], writes=[rkey])
                P.op(ACT, act(out_rstd, out_rstd, AF.Exp, scale=-0.5), reads=[rkey], writes=[rkey])

            def pre_norm(X, UT, gcol, bank_i):
                for c in range(8):
                    eng = ACT if c % 2 == 0 else POOL
                    if eng == ACT:
                        P.op(ACT, act(UT[:, c, :], X[:, c, :], AF.Square), reads=[("X", c)], writes=[("UT", c)])
                    else:
                        P.op(POOL, tt(UT[:, c, :], X[:, c, :], X[:, c, :], ALU.mult), reads=[("X", c)], writes=[("UT", c)])
                rstd_from_sq(UT, TT, D, bank_i, RSTD, [("UT", c) for c in range(8)])
                for c in range(8):
                    P.op(DVE, stt(UT[:, c, :], X[:, c, :], PPt[:, gcol + c:gcol + c + 1], RSTD, ALU.mult, ALU.mult),
                         reads=[("X", c), "RSTD", "PP"], writes=[("UT", c)])

            def post_norm_add(Y, ykeys, X, SQ, sqkeyname, gcol, bank_i):
                for j in range(8):
                    P.op(POOL, tt(SQ[:, j, :], Y[:, j, :], Y[:, j, :], ALU.mult), reads=[ykeys[j]], writes=[(sqkeyname, j)])
                rstd_from_sq(SQ, TT, D, bank_i, RSTD, [(sqkeyname, j) for j in range(8)])
                for j in range(8):
                    P.op(DVE, stt(Y[:, j, :], Y[:, j, :], PPt[:, gcol + j:gcol + j + 1], RSTD, ALU.mult, ALU.mult),
                         reads=[ykeys[j], "RSTD", "PP"], writes=[ykeys[j]])
                    P.op(POOL, tt(X[:, j, :], X[:, j, :], Y[:, j, :], ALU.add), reads=[ykeys[j], ("X", j)], writes=[("X", j)])

            def load_w(dst, src, kch, stream, key):
                sv = src.rearrange("(k p) n -> p k n", p=128)
                for k in range(kch):
                    P.dma(POOL, stream, dma(dst[:, k, :], sv[:, k, :]), writes=[key])

            def cvt_w(nm, src, rows):
                for k in range(rows // 128):
                    P.dma(POOL, "cvt", dma(wb[nm][k * 128:(k + 1) * 128, :], src[k * 128:(k + 1) * 128, :]), writes=["b_" + nm])

            def load_wb(dst, nm, kch, stream, key):
                sv = wb[nm].rearrange("(k p) n -> p k n", p=128)
                for k0 in range(0, kch, 8):
                    k1 = min(kch, k0 + 8)
                    P.dma(SP, stream, dma(dst[:, k0:k1, :], sv[:, k0:k1, :]), reads=["b_" + nm], writes=[key])

            AR.reset(base_off)
            WIN = AR.alloc([128, 8, NIN], BF16)
            WOUT = AR.alloc([128, 8, D], BF16)
            X = AR.alloc([128, 8, TT], F32)
            UT = AR.alloc([128, 8, TT], BF16)
            QA = AR.alloc([128, 4, TT], BF16)
            KK = AR.alloc([128, 2, 128 + TT], BF16)
            VAUG = AR.alloc([128, 10, 2, 128], BF16)
            QFKF = AR.alloc([128, 8, TT], F32)
            XR = AR.alloc([128, 8, TT], F32)
            SG = XR[:, 0:4, :]
            BT = XR[:, 4:8, :]
            SQ2 = AR.alloc([128, 8, 512], BF16)
            VT = SQ2[:, 0:4, :]
            KHAT = SQ2[:, 4:8, :]
            SGT = AR.alloc([128, 512], F32)
            LOGF = AR.alloc([128, 512], F32)
            KFT = AR.alloc([128, 512], F32)
            EXRB = SGT
            TMB = [(SGT, LOGF, KFT), tuple(AR.alloc([128, 512], F32) for _ in range(3))]
            E1 = [SGT, LOGF]
            G = KFT
            GINV = G
            QT = [AR.alloc([128, TT], BF16) for _ in range(4)]
            KTL = [AR.alloc([128, TT], BF16) for _ in range(4)]
            QH = [AR.alloc([128, TT], BF16) for _ in range(4)]
            EM = AR.alloc([128, 4, 8], F32)
            DEC = AR.alloc([128, 4, 8], F32)
            EMP = AR.alloc([128, 4, 8], F32)
            AT = AR.alloc([128, 4, 4, 64], BF16)
            S32 = AR.alloc([128, 4, 128], F32)
            SBF = AR.alloc([128, 2, 4, 128], BF16)
            OSB = [AR.alloc([128, TT], F32) for _ in range(2)]
            OSQ = AR.alloc([128, 2, TT], BF16)
            RSO = [AR.alloc([128, TT], F32) for _ in range(2)]
            CAT = AR.alloc([128, 8, TT], BF16)
            PT = [AR.alloc([128, 384], BF16) for _ in range(4)]
            RR = [AR.alloc([128, 512], F32) for _ in range(2)]
            ESK = AR.alloc([128, 2, 512], F32)
            LBB = AR.alloc([128, 3, 512], F32)
            print("phase A sbuf bytes", AR.off)
            QF = QFKF[:, 0:4]
            KF = QFKF[:, 4:8]
            Y = QFKF

            WBLK = [(0, 768), (768, 1792), (1792, 2304), (2304, NIN)]
            w_in_v = w_in.rearrange("(k p) n -> p k n", p=128)
            for bi, (c0, c1) in enumerate(WBLK):
                P.dma(POOL, "win%d" % bi, dma(WIN[:, :, c0:c1], w_in_v[:, :, c0:c1]), writes=[("WIN", bi)])

            def wkey(col):
                for bi, (c0, c1) in enumerate(WBLK):
                    if c0 <= col < c1:
                        return ("WIN", bi)
            load_w(WOUT, w_out, 8, "wout", "WOUT")
            cvt_jobs = []
            for nm, src, rows in (("wk", wk, D), ("wv", wv, D), ("wq", wq, D), ("wo", wo, D),
                                  ("w_gate", w_gate, D), ("w_up", w_up, D), ("w_down", w_down, DFF)):
                for k in range(rows // 128):
                    cvt_jobs.append((nm, src, k))

            def emit_cvt(n):
                for _ in range(n):
                    if cvt_jobs:
                        nm, src, k = cvt_jobs.pop(0)
                        P.dma(POOL, "cvt", dma(wb[nm][k * 128:(k + 1) * 128, :], src[k * 128:(k + 1) * 128, :]), writes=["b_" + nm])

            P.op(POOL, mset(VAUG, 1.0), writes=["VAUG"])
            P.op(POOL, mset(S32, 0.0), writes=[("S32", h) for h in range(4)])
            P.op(POOL, mset(SBF, 0.0), writes=[("SBF", h, p) for h in range(4) for p in range(2)])
            P.op(POOL, mset(KK, 0.0), writes=["KK"])
            P.dma(SP, "c2", dma(LBB[:, 0:2, :].rearrange("p a b -> p (a b)"), lbrow_d.partition_broadcast(128)), writes=["LBB"])
            P.op(DVE, tt(LBB[:, 2, :], LBB[:, 1, :], LBB[:, 0, :], ALU.subtract), reads=["LBB"], writes=["LBB"])
            P.op(ACT, act(LBB[:, 2, :], LBB[:, 2, :], AF.Exp), reads=["LBB"], writes=["LBB"])
            P.op(DVE, lambda e: e.tensor_scalar_add(out=LBB[:, 2, :], in0=LBB[:, 2, :], scalar1=1.0), reads=["LBB"], writes=["LBB"])
            P.op(DVE, recip(LBB[:, 0, :], LBB[:, 2, :]), reads=["LBB"], writes=["LBB"])
            P.op(DVE, ts(LBB[:, 1, :], LBB[:, 0, :], -1.0, 1.0, ALU.mult, ALU.add), reads=["LBB"], writes=["LBB"])
            LB_BC = LBB[:, 0, :]
            OML_BC = LBB[:, 1, :]
            for g in range(2):
                for cc in range(2):
                    P.op(DVE, cp(ESK[:, g, cc * 256:(cc + 1) * 256].rearrange("p (h q) -> p h q", h=4),
                                 ES[:, 4 * g:4 * g + 4].unsqueeze(2).to_broadcast([128, 4, 64])), reads=["ES"], writes=["ESK"])

            ckpt('setup')
            xv = dview(xT)
            h1v = dview(h1T)
            h2v = dview(h2T)
            ov = dview(outT)

            xrk = [("SG", j) for j in range(4)] + [("BT", j) for j in range(4)]
            sq2k = [("VT", j) for j in range(4)] + [("KHAT", j) for j in range(4)]
            P.dma(SP, "x", dma(X, xv[:, :, 0:TT]), writes=[("X", c) for c in range(8)])
            pre_norm(X, UT, G_MIX_PRE, 7)
            for t in range(NT):
                tok = slice(t * TT, (t + 1) * TT)
                if t + 1 < NT:
                    P.dma(SP, "x", dma(X, xv[:, :, (t + 1) * TT:(t + 2) * TT]), writes=[("X", c) for c in range(8)])
                ckpt('prenorm')
                utk = [("UT", k) for k in range(8)]
                for j in range(FM_CH):
                    ckpt(f'fm{j}')
                    b, bkey = bk(j % 2)
                    P.op(PE, group([mm(b[:, :], WIN[:, k, j * 128:(j + 1) * 128], UT[:, k, :], k == 0, k == 7) for k in range(8)]),
                         reads=utk + [wkey(j * 128)], writes=[bkey])
                    if j < 4:
                        P.op(ACT if j % 2 == 0 else DVE, (act(QA[:, j, :], b[:, :], AF.Copy) if j % 2 == 0 else cp(QA[:, j, :], b[:, :])),
                             reads=[bkey], writes=["QA"])
                    elif j < 6:
                        P.op(DVE, cp(KK[:, j - 4, 128:128 + TT], b[:, :]), reads=[bkey], writes=["KK"])
                    elif j < 10:
                        P.op(ACT, act(QF[:, j - 6, :], b[:, :], AF.Silu), reads=[bkey], writes=[("QFKF", j - 6)])
                    elif j < 14:
                        h = j - 10
                        P.op(ACT, act(KF[:, h, :], b[:, :], AF.Sigmoid), reads=[bkey], writes=[("QFKF", 4 + h)])
                        P.op(DVE, ts(KF[:, h, :], KF[:, h, :], LBP[:, 8 + h:9 + h], LBP[:, 4 + h:5 + h], ALU.mult, ALU.add),
                             reads=[("QFKF", 4 + h), "LBP"], writes=[("QFKF", 4 + h)])
                    else:
                        P.op(ACT, act(SG[:, j - 14, :], b[:, :], AF.Silu), reads=[bkey], writes=[("SG", j - 14)])
                ckpt('fm')
                for half in range(2):
                    b, bkey = bk(2 + half)
                    fns = []
                    for cc in range(4):
                        ch = half * 4 + cc
                        for k in range(8):
                            fns.append(mm(b[0:64, cc * 128:(cc + 1) * 128], UT[:, k, ch * 64:(ch + 1) * 64],
                                          WIN[:, k, VCOL:VCOL + 128], k == 0, k == 7))
                    P.op(PE, group(fns), reads=utk + [wkey(VCOL)], writes=[bkey])
                    P.op(DVE, cp(VAUG[0:64, 2 + half * 4:6 + half * 4, :, 0:64],
                                 b[0:64, :].rearrange("p (c g d) -> p c g d", c=4, g=2)), reads=[bkey], writes=["VAUG"])
                ckpt('tmv')
                tm_steps = []

                def tm_block(s):
                    tsl = slice(s * 128, (s + 1) * 128)
                    sgt, logf, kft = TMB[s % 2]
                    k0, k1, k2 = ("T0", "T1", "T2") if s % 2 == 0 else ("T0b", "T1b", "T2b")
                    bF, kF = bk(6)
                    bI, kI = bk(7)
                    bB, kB = bk(6)
                    bR, kR = bk(7)

                    def stA():
                        P.op(PE, group([mm(bF[:, :], UT[:, k, tsl], WIN[:, k, FTM:FTM + 512], k == 0, k == 7) for k in range(8)]),
                             reads=utk + [wkey(FTM)], writes=[kF])
                        P.op(PE, group([mm(bI[:, :], UT[:, k, tsl], WIN[:, k, ITM:ITM + 512], k == 0, k == 7) for k in range(8)]),
                             reads=utk + [wkey(ITM)], writes=[kI])

                    def stB():
                        P.op(ACT, act(sgt, bF[:, :], AF.Sigmoid), reads=[kF], writes=[k0])
                        P.op(DVE, cp(VT[:, s, :], bI[:, :]), reads=[kI], writes=[("VT", s)])

                    def stC():
                        P.op(DVE, tt(sgt, sgt, OML_BC, ALU.mult), reads=[k0, "LBB"], writes=[k0])
                        P.op(DVE, tt(sgt, sgt, LB_BC, ALU.add), reads=[k0, "LBB"], writes=[k0])

                    def stD():
                        P.op(ACT, act(logf, sgt, AF.Ln), reads=[k0], writes=[k1])
                        P.op(POOL, ts(kft, sgt, -1.0, 1.0, ALU.mult, ALU.add), reads=[k0], writes=[k2])

                    def stE():
                        P.op(PE, group([mm(bB[:, h * 128:(h + 1) * 128], logf[:, h * 128:(h + 1) * 128], TRIINC, True, True) for h in range(4)]),
                             reads=[k1, "CON"], writes=[kB])

                    def stF():
                        P.op(DVE, cp(BT[:, :, tsl], bB[:, :].rearrange("p (h t) -> p h t", h=4)), reads=[kB], writes=[("BT", h) for h in range(4)])
                        P.op(PE, mm(bR[:, :], TRIREV, logf, True, True), reads=[k1, "CON"], writes=[kR])

                    def stG():
                        P.op(ACT, act(sgt, bR[:, :], AF.Exp), reads=[kR], writes=[k0])
                        P.op(POOL, tt(KHAT[:, s, :], kft, sgt, ALU.mult), reads=[k2, k0], writes=[("KHAT", s)])
                    return [stA, stB, stC, stD, stE, stF, stG]

                for s0 in (0, 2):
                    a_, b_ = tm_block(s0), tm_block(s0 + 1)
                    tm_steps.extend([a_[0], a_[1], b_[0], b_[1], a_[2], b_[2], a_[3], b_[3],
                                     a_[4], a_[5], a_[6], b_[4], b_[5], b_[6]])

                ckpt('tmh')
                units = []
                for g in range(2):
                    for c2 in range(4):
                        for cc in range(2):
                            for i2 in range(2):
                                units.append((g, c2, cc, i2))

                def emit_S(u):
                    g, c2, cc, i2 = units[u]
                    c = c2 * 2 + cc
                    gc = t * 8 + c
                    parts = [p for p in range(3) if gc - 2 + p >= 0]
                    bS, kS = bk(i2 * 2 + (u // 2) % 2)
                    pt = PT[u % 4]
                    ptk = ("PT", u % 4)
                    rows = slice(i2 * 64, i2 * 64 + 64)
                    fns = []
                    for p in parts:
                        kc = c + p
                        for pr in range(2):
                            fns.append(mm(bS[0:64, (p * 2 + pr) * 64:(p * 2 + pr + 1) * 64],
                                          KK[rows, g, kc * 64:(kc + 1) * 64],
                                          QA[rows, 2 * g + pr, c * 64:(c + 1) * 64], True, True))
                    P.op(PE, group(fns), reads=["KK", "QA"], writes=[kS])
                    lo = parts[0] * 128
                    P.op(ACT, act(pt[0:64, lo:384], bS[0:64, lo:384], AF.Exp, scale=0.125), reads=[kS], writes=[ptk])

                def emit_PV(u):
                    g, c2, cc, i2 = units[u]
                    c = c2 * 2 + cc
                    gc = t * 8 + c
                    parts = [p for p in range(3) if gc - 2 + p >= 0]
                    grp = g * 4 + c2
                    bO, kO = bk(4 + grp % 2)
                    pt = PT[u % 4]
                    ptk = ("PT", u % 4)
                    fns = []
                    slot = (cc * 2 + i2) * 128
                    for ii, p in enumerate(parts):
                        kc = c + p
                        fns.append(mm(bO[:, slot:slot + 128], VAUG[0:64, kc, g, :], pt[0:64, p * 128:(p + 1) * 128],
                                      ii == 0, ii == len(parts) - 1))
                    P.op(PE, group(fns), reads=["VAUG", ptk], writes=[kO])
                    if cc == 1 and i2 == 1:
                        def evac(g=g, c2=c2, grp=grp, bO=bO, kO=kO):
                            r = RR[grp % 2]
                            rk = ("RR", grp % 2)
                            P.op(DVE, tt(r[0:64, :], bO[64:128, :], ESK[64:128, g, :], ALU.add), reads=[kO, "ESK"], writes=[rk])
                            P.op(ACT, act(r[0:64, :], r[0:64, :], AF.Ln), reads=[rk], writes=[rk])
                            P.op(ACT, act(r[0:64, :], r[0:64, :], AF.Exp, scale=-1.0), reads=[rk], writes=[rk])
                            ov4 = bO[0:64, :].rearrange("p (cc i pr q) -> p i pr cc q", cc=2, pr=2, i=2)
                            rv4 = r[0:64, :].rearrange("p (cc i pr q) -> p i pr cc q", cc=2, pr=2, i=2)
                            for j2 in range(2):
                                dst = CAT[j2 * 64:(j2 + 1) * 64, 2 * g:2 * g + 2, c2 * 128:(c2 + 1) * 128].rearrange("p a (cc q) -> p a cc q", cc=2)
                                P.op(DVE, tt(dst, ov4[:, j2, :, :, :], rv4[:, j2, :, :, :], ALU.mult), reads=[kO, rk],
                                     writes=[("CAT", 2 * g), ("CAT", 2 * g + 1)])
                        pending.append((u + 2, evac))

                PD = 3
                pending = []
                for u in range(min(PD, len(units))):
                    emit_S(u)
                for u in range(len(units)):
                    if u + PD < len(units):
                        emit_S(u + PD)
                    if tm_steps:
                        tm_steps.pop(0)()
                    emit_PV(u)
                    while pending and pending[0][0] <= u:
                        pending.pop(0)[1]()
                while tm_steps:
                    tm_steps.pop(0)()
                while pending:
                    pending.pop(0)[1]()
                ckpt('swa')
                P.op(POOL, cp(VAUG[0:64, 0:2, :, 0:64], VAUG[0:64, 8:10, :, 0:64]), reads=["VAUG"], writes=["VAUG"])
                P.op(POOL, cp(KK[:, :, 0:128], KK[:, :, TT:TT + 128]), reads=["KK"], writes=["KK"])

                ckpt('carry')
                for h in range(4):
                    hb = h % 2
                    e1 = (TMB[0][0], TMB[0][1], TMB[1][0], TMB[1][1])[h]
                    e1k = ("T0", "T1", "T0b", "T1b")[h]
                    G = TMB[hb][2]
                    GINV = G
                    gk = "T2" if hb == 0 else "T2b"
                    btk = ("BT", h)
                    P.op(ACT, act(e1, BT[:, h, :], AF.Exp), reads=[btk], writes=[e1k])
                    P.op(ACT, act(EM[:, h, :], BT[:, h, :].rearrange("p (c t) -> p c t", t=64)[:, :, 31], AF.Exp, scale=-1.0),
                         reads=[btk], writes=[("EM", h)])
                    e1v = e1.rearrange("p (c t) -> p c t", t=64)
                    P.op(POOL, cp(DEC[:, h, :], e1v[:, :, 63]), reads=[e1k], writes=[("DEC", h)])
                    P.op(DVE, tt(G.rearrange("p (c t) -> p c t", t=64), e1v, EM[:, h, :].unsqueeze(2).to_broadcast([128, 8, 64]), ALU.mult),
                         reads=[e1k, ("EM", h)], writes=[gk])
                    P.op(POOL, tt(QH[h], QF[:, h, :], e1, ALU.mult), reads=[("QFKF", h), e1k], writes=[("QH", h)])
                    P.op(DVE, tt(QT[h], QF[:, h, :], G, ALU.mult), reads=[("QFKF", h), gk], writes=[("QT", h)])
                    P.op(ACT, act(GINV, BT[:, h, :], AF.Exp, scale=-1.0), reads=[btk, gk], writes=[gk])
                    P.op(ACT, act(EMP[:, h, :], BT[:, h, :].rearrange("p (c t) -> p c t", t=64)[:, :, 31], AF.Exp),
                         reads=[btk], writes=[("EMP", h)])
                    P.op(DVE, tt(GINV.rearrange("p (c t) -> p c t", t=64), GINV.rearrange("p (c t) -> p c t", t=64),
                                 EMP[:, h, :].unsqueeze(2).to_broadcast([128, 8, 64]), ALU.mult), reads=[gk, ("EMP", h)], writes=[gk])
                    P.op(POOL, tt(KTL[h], KF[:, h, :], GINV, ALU.mult), reads=[("QFKF", 4 + h), gk], writes=[("KTL", h)])
                    bA, kA = bk(h)
                    P.op(PE, group([mm(bA[0:64, c * 64:(c + 1) * 64], KTL[h][:, c * 64:(c + 1) * 64], QT[h][:, c * 64:(c + 1) * 64], True, True)
                                    for c in range(8)]), reads=[("KTL", h), ("QT", h)], writes=[kA])
                    bAv = bA[0:64, :].rearrange("p (s r q) -> p s r q", s=4, r=2)
                    for par in range(2):
                        P.op(DVE, tt(AT[par * 64:(par + 1) * 64, h, :, :], bAv[:, :, par, :],
                                     MASK[0:64, :].unsqueeze(1).to_broadcast([64, 4, 64]), ALU.mult),
                             reads=[kA, "CON"], writes=[("AT", h)])

                def emit_kv(c):
                    b, bkey = bk(4 + c % 2)
                    rows = slice((c % 2) * 64, (c % 2) * 64 + 64)
                    P.op(PE, group([mm(b[:, h * 128:(h + 1) * 128], KHAT[rows, c // 2, h * 128:(h + 1) * 128],
                                       VT[rows, c // 2, h * 128:(h + 1) * 128], True, True) for h in range(4)]),
                         reads=[("KHAT", c // 2), ("VT", c // 2)], writes=[bkey])

                emit_kv(0)
                for c in range(8):
                    if c + 1 < 8:
                        emit_kv(c + 1)
                    rows = slice((c % 2) * 64, (c % 2) * 64 + 64)
                    gc = t * 8 + c
                    cur = gc % 2
                    bkv, kkv = bk(4 + c % 2)
                    for h in range(4):
                        bO, kO = bk(h)
                        P.op(PE, group([mm(bO[:, c * 64:(c + 1) * 64], VT[rows, c // 2, h * 128:(h + 1) * 128], AT[rows, h, c // 2, :], True, False),
                                        mm(bO[:, c * 64:(c + 1) * 64], SBF[:, cur, h, :], QH[h][:, c * 64:(c + 1) * 64], False, True)]),
                             reads=[("VT", c // 2), ("AT", h), ("SBF", h, cur), ("QH", h)], writes=[kO])
                        P.op(DVE, stt(S32[:, h, :], S32[:, h, :], DEC[:, h, c:c + 1], bkv[:, h * 128:(h + 1) * 128], ALU.mult, ALU.add),
                             reads=[("S32", h), ("DEC", h), kkv], writes=[("S32", h)])
                        if h % 2 == 0:
                            P.op(ACT, act(SBF[:, 1 - cur, h, :], S32[:, h, :], AF.Copy), reads=[("S32", h)], writes=[("SBF", h, 1 - cur)])
                        else:
                            P.op(POOL, cp(SBF[:, 1 - cur, h, :], S32[:, h, :]), reads=[("S32", h)], writes=[("SBF", h, 1 - cur)])
                t1b_bf = TMB[1][1].bitcast(BF16)
                osb4 = [(OSB[0], ("OSB", 0)), (OSB[1], ("OSB", 1)), (TMB[0][0], "T0"), (TMB[0][1], "T1")]
                rso4 = [(RSO[0], ("RSO", 0)), (RSO[1], ("RSO", 1)), (TMB[0][2], "T2"), (TMB[1][0], "T0b")]
                osq4 = [(OSQ[:, 0:1, :], ("OSQ", 0)), (OSQ[:, 1:2, :], ("OSQ", 1)),
                        (t1b_bf[:, 0:512].rearrange("p (a b) -> p a b", a=1), "T1b"),
                        (t1b_bf[:, 512:1024].rearrange("p (a b) -> p a b", a=1), "T1b")]
                for h in range(4):
                    bO, kO = bk(h)
                    P.op(ACT, act(osb4[h][0], bO[:, :], AF.Copy, scale=128.0 ** -0.5), reads=[kO], writes=[osb4[h][1]])
                for h in range(4):
                    P.op(POOL, tt(osq4[h][0][:, 0, :], osb4[h][0], osb4[h][0], ALU.mult), reads=[osb4[h][1]], writes=[osq4[h][1]])
                for h in range(4):
                    b, bkey = bk(4 + h)
                    P.op(PE, mm(b[:, :], ONES, osq4[h][0][:, 0, :], True, True), reads=[osq4[h][1], "ONES"], writes=[bkey])
                for h in range(4):
                    b, bkey = bk(4 + h)
                    P.op(ACT, act(rso4[h][0], b[:, :], AF.Ln, bias=EPS, scale=1.0 / 128), reads=[bkey], writes=[rso4[h][1]])
                for h in range(4):
                    P.op(ACT, act(rso4[h][0], rso4[h][0], AF.Exp, scale=-0.5), reads=[rso4[h][1]], writes=[rso4[h][1]])
                for h in range(4):
                    P.op(DVE, stt(osb4[h][0], osb4[h][0], PPt[:, ONG:ONG + 1], rso4[h][0], ALU.mult, ALU.mult),
                         reads=[osb4[h][1], rso4[h][1], "PP"], writes=[osb4[h][1]])
                for h in range(4):
                    P.op(POOL, tt(CAT[:, 4 + h, :], osb4[h][0], SG[:, h, :], ALU.mult), reads=[osb4[h][1], ("SG", h)], writes=[("CAT", 4 + h)])

                ckpt('hgrn')
                P.dma(SP, "xr", dma(XR, xv[:, :, tok]), writes=xrk)
                catk = [("CAT", k) for k in range(8)]
                for j in range(8):
                    b, bkey = bk(j % 2)
                    P.op(PE, group([mm(b[:, :], WOUT[:, k, j * 128:(j + 1) * 128], CAT[:, k, :], k == 0, k == 7) for k in range(8)]),
                         reads=catk + ["WOUT"], writes=[bkey])
                    P.op(ACT, act(Y[:, j, :], b[:, :], AF.Copy), reads=[bkey], writes=[("QFKF", j)])
                    P.op(POOL, tt(SQ2[:, j, :], Y[:, j, :], Y[:, j, :], ALU.mult), reads=[("QFKF", j)], writes=[sq2k[j]])
                if t + 1 < NT:
                    pre_norm(X, UT, G_MIX_PRE, 7)
                rstd_from_sq(SQ2, TT, D, 6, RSO[0], sq2k, rkey=("RSO", 0))
                for j in range(8):
                    P.op(DVE, stt(Y[:, j, :], Y[:, j, :], PPt[:, G_MIX_POST + j:G_MIX_POST + j + 1], RSO[0], ALU.mult, ALU.mult),
                         reads=[("QFKF", j), ("RSO", 0), "PP"], writes=[("QFKF", j)])
                    P.op(POOL, tt(XR[:, j, :], XR[:, j, :], Y[:, j, :], ALU.add), reads=[("QFKF", j), xrk[j]], writes=[xrk[j]])
                P.dma(SP, "h1", dma(h1v[:, :, tok], XR), reads=xrk, writes=[("h1", t)])
                emit_cvt(10)

            emit_cvt(len(cvt_jobs))
            ckpt('A')
            TB = 256
            NTB = T // TB

            def rstd2(SQ, sqkeys, dim, bank_i, out_rstd, rkey, nfree):
                b, bkey = bk(bank_i)
                nch = SQ.shape[1]
                P.op(PE, group([mm(b[:, 0:nfree], ONES, SQ[:, c, :], c == 0, c == nch - 1) for c in range(nch)]),
                     reads=list(sqkeys) + ["ONES"], writes=[bkey])
                P.op(ACT, act(out_rstd, b[:, 0:nfree], AF.Ln, bias=EPS, scale=1.0 / dim), reads=[bkey], writes=[rkey])
                P.op(ACT, act(out_rstd, out_rstd, AF.Exp, scale=-0.5), reads=[rkey], writes=[rkey])

            def pre_norm2(Xb, xk, Ub, uk, gcol, bank_i, R, rkey):
                for c in range(8):
                    if c % 2 == 0:
                        P.op(ACT, act(Ub[:, c, :], Xb[:, c, :], AF.Square), reads=[(xk, c)], writes=[(uk, c)])
                    else:
                        P.op(POOL, tt(Ub[:, c, :], Xb[:, c, :], Xb[:, c, :], ALU.mult), reads=[(xk, c)], writes=[(uk, c)])
                rstd2(Ub, [(uk, c) for c in range(8)], D, bank_i, R, rkey, TB)
                for c in range(8):
                    P.op(DVE, stt(Ub[:, c, :], Xb[:, c, :], PPt[:, gcol + c:gcol + c + 1], R, ALU.mult, ALU.mult),
                         reads=[(xk, c), rkey, "PP"], writes=[(uk, c)])

            def post_norm2(Yb, yk, Xb, xk, SQb, sk, gcol, bank_i, R, rkey):
                rstd2(SQb, [(sk, j) for j in range(8)], D, bank_i, R, rkey, TB)
                for j in range(8):
                    P.op(DVE, stt(Yb[:, j, :], Yb[:, j, :], PPt[:, gcol + j:gcol + j + 1], R, ALU.mult, ALU.mult),
                         reads=[(yk, j), rkey, "PP"], writes=[(yk, j)])
                    P.op(POOL, tt(Xb[:, j, :], Xb[:, j, :], Yb[:, j, :], ALU.add), reads=[(yk, j), (xk, j)], writes=[(xk, j)])

            P.barrier()
            TOP = NBYTES - 2 * 8 * DFF * 2
            AR.reset(TOP)
            WG = AR.alloc([128, 8, DFF], BF16)
            WU = AR.alloc([128, 8, DFF], BF16)
            AR.reset(TOP)
            WK = AR.alloc([128, 8, D], BF16)
            WV = AR.alloc([128, 8, D], BF16)
            AR.reset(base_off)
            WQ = AR.alloc([128, 8, D], BF16)
            WO = AR.alloc([128, 8, D], BF16)
            KTm = AR.alloc([128, 8, MEM], BF16)
            VX = AR.alloc([128, 2, D], BF16)
            MT = AR.alloc([128, 8, MEM], BF16)
            XB = [AR.alloc([128, 8, TB], F32) for _ in range(3)]
            UB = [AR.alloc([128, 8, TB], BF16) for _ in range(2)]
            MEMT = XB[2]
            MSQ = UB[0]
            memk = [(("X", 2), c) for c in range(8)]
            msqk = [(("U", 0), c) for c in range(8)]
            QXB = [AR.alloc([128, 8, TB], BF16) for _ in range(2)]
            PX = AR.alloc([128, 4, 2, TB], BF16)
            O2 = AR.alloc([128, 8, TB], BF16)
            Y = AR.alloc([128, 8, TB], F32)
            SQ = AR.alloc([128, 8, TB], BF16)
            RD = [AR.alloc([128, TB], F32) for _ in range(2)]
            RPRE = AR.alloc([128, TB], F32)
            RPOST = AR.alloc([128, TB], F32)
            print("phase B sbuf bytes", AR.off, "top", TOP)
            assert AR.off <= TOP
            def b_ld(t):
                tok = slice(t * TB, (t + 1) * TB)
                xb, xk = XB[t % 3], ("X", t % 3)
                P.dma(SP, "xb%d" % (t % 3), dma(xb, h1v[:, :, tok]), writes=[(xk, c) for c in range(8)])

            P.dma(SP, "mem", dma(MEMT, dview(memT)), writes=memk)
            load_wb(WK, "wk", 8, "wk", "WKV")
            load_wb(WV, "wv", 8, "wv", "WKV")
            b_ld(0)
            load_wb(WQ, "wq", 8, "wq", "WQ")
            if NTB > 1:
                b_ld(1)
            load_wb(WO, "wo", 8, "wo", "WO")
            for c in range(8):
                P.op(POOL, tt(MSQ[:, c, :], MEMT[:, c, :], MEMT[:, c, :], ALU.mult), reads=memk, writes=msqk)
            rstd_from_sq(MSQ, MEM, D, 7, RSTD[:, 0:MEM], msqk)
            for c in range(8):
                P.op(DVE, stt(MT[:, c, :], MEMT[:, c, :], PPt[:, G_MEM + c:G_MEM + c + 1], RSTD[:, 0:MEM], ALU.mult, ALU.mult),
                     reads=memk + ["RSTD", "PP"], writes=["MT"])
            for j in range(8):
                b, bkey = bk(j % 2)
                P.op(PE, group([mm(b[:, 0:MEM], WK[:, k, j * 128:(j + 1) * 128], MT[:, k, :], k == 0, k == 7) for k in range(8)]),
                     reads=["WKV", "MT"], writes=[bkey])
                P.op(ACT, act(KTm[:, j, :], b[:, 0:MEM], AF.Copy), reads=[bkey], writes=["KTm"])
            for mc in range(2):
                for nh in range(2):
                    b, bkey = bk(2 + nh)
                    P.op(PE, group([mm(b[:, :], MT[:, k, mc * 128:(mc + 1) * 128], WV[:, k, nh * 512:(nh + 1) * 512], k == 0, k == 7) for k in range(8)]),
                         reads=["WKV", "MT"], writes=[bkey])
                    P.op(DVE, cp(VX[:, mc, nh * 512:(nh + 1) * 512], b[:, :]), reads=[bkey], writes=["VX"])

            def b_pre(t):
                ckpt('b_g1')
                xb, xk = XB[t % 3], ("X", t % 3)
                ub, uk = UB[t % 2], ("U", t % 2)
                pre_norm2(xb, xk, ub, uk, G_X_PRE, 7, RPRE, "RPRE")

            def b_q(t):
                ckpt('b_q')
                ub, uk = UB[t % 2], ("U", t % 2)
                utk = [(uk, k) for k in range(8)]
                qx, qk = QXB[t % 2], ("QX", t % 2)
                for j in range(8):
                    b, bkey = bk(6 + j % 2)
                    P.op(PE, group([mm(b[:, 0:TB], WQ[:, k, j * 128:(j + 1) * 128], ub[:, k, :], k == 0, k == 7) for k in range(8)]),
                         reads=utk + ["WQ"], writes=[bkey])
                    if j % 2 == 0:
                        P.op(ACT, act(qx[:, j, :], b[:, 0:TB], AF.Copy), reads=[bkey], writes=[(qk, j)])
                    else:
                        P.op(DVE, cp(qx[:, j, :], b[:, 0:TB]), reads=[bkey], writes=[(qk, j)])

            def b_g2(t):
                tok = slice(t * TB, (t + 1) * TB)
                xb, xk = XB[t % 3], ("X", t % 3)
                qx, qk = QXB[t % 2], ("QX", t % 2)

                def scores(h):
                    ckpt('b_sc')
                    for mc in range(2):
                        b, bkey = bk((h % 2) * 2 + mc)
                        P.op(PE, group([mm(b[:, 0:TB], KTm[:, 2 * h + kk, mc * 128:(mc + 1) * 128], qx[:, 2 * h + kk, :], kk == 0, kk == 1) for kk in range(2)]),
                             reads=["KTm", (qk, 2 * h), (qk, 2 * h + 1)], writes=[bkey])
                        P.op(ACT, act(PX[:, h, mc, :], b[:, 0:TB], AF.Exp, scale=1.0 / 16.0), reads=[bkey], writes=[("PX", h)])

                def pv(h):
                    ckpt('b_pv')
                    bD, kD = bk(4)
                    P.op(PE, group([mm(bD[:, 0:TB], ONES, PX[:, h, mc, :], mc == 0, mc == 1) for mc in range(2)]),
                         reads=["ONES", ("PX", h)], writes=[kD])
                    rd = RD[h % 2]
                    rdk = ("RD", h % 2)
                    P.op(ACT, act(rd, bD[:, 0:TB], AF.Ln), reads=[kD], writes=[rdk])
                    P.op(ACT, act(rd, rd, AF.Exp, scale=-1.0), reads=[rdk], writes=[rdk])
                    for dd in range(2):
                        j = 2 * h + dd
                        b, bkey = bk(5 + dd)
                        P.op(PE, group([mm(b[:, 0:TB], VX[:, mc, j * 128:(j + 1) * 128], PX[:, h, mc, :], mc == 0, mc == 1) for mc in range(2)]),
                             reads=["VX", ("PX", h)], writes=[bkey])
                        P.op(DVE, tt(O2[:, j, :], b[:, 0:TB], rd, ALU.mult), reads=[bkey, rdk], writes=[("O2", j)])

                scores(0)
                for h in range(4):
                    if h + 1 < 4:
                        scores(h + 1)
                    pv(h)

            def b_wo(t):
                ckpt('b_wo')
                o2k = [("O2", k) for k in range(8)]
                for j in range(8):
                    b, bkey = bk(j % 2)
                    P.op(PE, group([mm(b[:, 0:TB], WO[:, k, j * 128:(j + 1) * 128], O2[:, k, :], k == 0, k == 7) for k in range(8)]),
                         reads=o2k + ["WO"], writes=[bkey])
                    P.op(ACT, act(Y[:, j, :], b[:, 0:TB], AF.Copy), reads=[bkey], writes=[("Y", j)])
                    P.op(POOL, tt(SQ[:, j, :], Y[:, j, :], Y[:, j, :], ALU.mult), reads=[("Y", j)], writes=[("SQ", j)])

            def b_post(t):
                ckpt('b_post')
                tok = slice(t * TB, (t + 1) * TB)
                xb, xk = XB[t % 3], ("X", t % 3)
                post_norm2(Y, "Y", xb, xk, SQ, "SQ", G_X_POST, 4, RPOST, "RPOST")
                P.dma(SP, "h2", dma(h2v[:, :, tok], xb), reads=[(xk, c) for c in range(8)], writes=[("h2", t)])

            b_pre(0)
            b_q(0)
            for t in range(NTB):
                if t + 2 < NTB:
                    b_ld(t + 2)
                if t == 1:
                    load_wb(WG, "w_gate", 8, "wg", "WKV")
                    load_wb(WU, "w_up", 8, "wu", "WKV")
                b_g2(t)
                if t + 1 < NTB:
                    b_pre(t + 1)
                b_wo(t)
                if t + 1 < NTB:
                    b_q(t + 1)
                b_post(t)

            ckpt('B')
            P.barrier()
            AR.reset(base_off)
            WD = AR.alloc([128, 22, D], BF16)
            XB = [AR.alloc([128, 8, TB], F32) for _ in range(3)]
            UB = [AR.alloc([128, 8, TB], BF16) for _ in range(2)]
            ACTT = AR.alloc([128, 22, TB], BF16)
            Y = AR.alloc([128, 8, TB], F32)
            SQ = AR.alloc([128, 8, TB], BF16)
            GS = [AR.alloc([128, TB], F32) for _ in range(2)]
            RPRE = AR.alloc([128, TB], F32)
            RPOST = AR.alloc([128, TB], F32)
            print("phase C sbuf bytes", AR.off, "top", TOP)
            assert AR.off <= TOP
            def c_ld(t):
                tok = slice(t * TB, (t + 1) * TB)
                xb, xk = XB[t % 3], ("X", t % 3)
                P.dma(SP, "xb%d" % (t % 3), dma(xb, h2v[:, :, tok]), writes=[(xk, c) for c in range(8)])

            def c_g1(t):
                xb, xk = XB[t % 3], ("X", t % 3)
                pre_norm2(xb, xk, UB[t % 2], ("U", t % 2), G_FFN_PRE, 6, RPRE, "RPRE")

            def c_g2(t):
                tok = slice(t * TB, (t + 1) * TB)
                xb, xk = XB[t % 3], ("X", t % 3)
                ub, uk = UB[t % 2], ("U", t % 2)
                utk = [(uk, k) for k in range(8)]
                for j in range(22):
                    bG, kG = bk((2 * j) % 4)
                    bU, kU = bk((2 * j) % 4 + 1)
                    P.op(PE, group([mm(bG[:, 0:TB], WG[:, k, j * 128:(j + 1) * 128], ub[:, k, :], k == 0, k == 7) for k in range(8)]),
                         reads=utk, writes=[kG])
                    P.op(PE, group([mm(bU[:, 0:TB], WU[:, k, j * 128:(j + 1) * 128], ub[:, k, :], k == 0, k == 7) for k in range(8)]),
                         reads=utk, writes=[kU])
                    gs = GS[j % 2]
                    P.op(ACT, act(gs, bG[:, 0:TB], AF.Silu), reads=[kG], writes=[("GS", j % 2)])
                    P.op(DVE, tt(ACTT[:, j, :], bU[:, 0:TB], gs, ALU.mult), reads=[kU, ("GS", j % 2)], writes=[("ACTT", j)])

            def c_g2b(t):
                tok = slice(t * TB, (t + 1) * TB)
                xb, xk = XB[t % 3], ("X", t % 3)
                ak = [("ACTT", j) for j in range(22)]
                for jo in range(8):
                    b, bkey = bk(4 + jo % 2)
                    P.op(PE, group([mm(b[:, 0:TB], WD[:, k, jo * 128:(jo + 1) * 128], ACTT[:, k, :], k == 0, k == 21) for k in range(22)]),
                         reads=ak + ["WD"], writes=[bkey])
                    P.op(ACT, act(Y[:, jo, :], b[:, 0:TB], AF.Copy), reads=[bkey], writes=[("Y", jo)])
                    P.op(POOL, tt(SQ[:, jo, :], Y[:, jo, :], Y[:, jo, :], ALU.mult), reads=[("Y", jo)], writes=[("SQ", jo)])
                post_norm2(Y, "Y", xb, xk, SQ, "SQ", G_FFN_POST, 7, RPOST, "RPOST")
                P.dma(SP, "out", dma(ov[:, :, tok], xb), reads=[(xk, c) for c in range(8)], writes=[("out", t)])

            c_ld(0)
            if NTB > 1:
                c_ld(1)
            load_wb(WD, "w_down", 22, "wd", "WD")
            c_g1(0)
            for t in range(NTB):
                if t + 2 < NTB:
                    c_ld(t + 2)
                c_g2(t)
                if t + 1 < NTB:
                    c_g1(t + 1)
                c_g2b(t)
        except _Stop:
            pass
        P.barrier()
        P.wait_keys(SP, [("out", t) for t in range(T // 256)])
        P.replay(block)
        print("instr counts", {e: len(P.lists[e]) for e in ENGS})
        build.last_labels = P.labels
    return nc


def _consts():
    c = np.zeros((128, 320), np.float32)
    s = np.arange(128)[:, None]
    t = np.arange(128)[None, :]
    same = (s // 64) == (t // 64)
    c[:, 0:128] = (same & (s <= t)).astype(np.float32)
    c[:, 128:256] = (same & (s > t)).astype(np.float32)
    k = np.arange(64)[:, None]
    q = np.arange(64)[None, :]
    m = (k <= q).astype(np.float32)
    c[0:64, 256:320] = m
    c[64:128, 256:320] = m
    return c


def _win_perm():
    cols = list(range(0, 512))
    cols += list(range(512, 576)) * 2 + list(range(576, 640)) * 2
    cols += list(range(768, 1280)) + list(range(1280, 1792)) + list(range(2304, 2816))
    cols += list(range(640, 768))
    cols += list(range(1792, 2304))
    assert len(cols) == NIN
    return np.array(cols)


def _pp(inp):
    pp = np.zeros((128, NPP), np.float32)

    def pc(v):
        return np.ascontiguousarray(np.asarray(v, np.float32).reshape(8, 128).T)
    pp[:, G_MIX_PRE:G_MIX_PRE + 8] = pc(inp["g_mix_pre"][0])
    pp[:, G_MIX_POST:G_MIX_POST + 8] = pc(inp["g_mix_post"][0])
    pp[:, G_X_PRE:G_X_PRE + 8] = pc(inp["g_x_pre"][0])
    pp[:, G_X_POST:G_X_POST + 8] = pc(inp["g_x_post"][0])
    pp[:, G_FFN_PRE:G_FFN_PRE + 8] = pc(inp["g_ffn_pre"][0])
    pp[:, G_FFN_POST:G_FFN_POST + 8] = pc(inp["g_ffn_post"][0])
    pp[:, G_MEM:G_MEM + 8] = pc(inp["g_mem"][0])
    pp[:, ONG] = np.asarray(inp["hgrn_onorm"][0], np.float32)
    pp[:, LB0:LB0 + 4] = np.asarray(inp["hgrn_lb"][0], np.float32).reshape(4, 128).T
    pp[:, LB1:LB1 + 4] = np.asarray(inp["hgrn_lb"][1], np.float32).reshape(4, 128).T
    sk = np.asarray(inp["sinks"][0], np.float32)
    order = [4 * g + 2 * pr + i2 for g in range(2) for i2 in range(2) for pr in range(2)]
    pp[:, SNK:SNK + 8] = sk[order][None, :]
    return pp


def make_in_maps(inp, ncores, T):
    f = lambda a: np.ascontiguousarray(np.asarray(a, np.float32))
    shared = {
        "w_in": f(np.asarray(inp["w_in"][0])[:, _win_perm()]),
        "w_out": f(inp["w_out"][0]), "wq": f(inp["wq_x"][0]), "wk": f(inp["wk_x"][0]),
        "wv": f(inp["wv_x"][0]), "wo": f(inp["wo_x"][0]),
        "w_gate": f(inp["w_gate"][0]), "w_up": f(inp["w_up"][0]), "w_down": f(inp["w_down"][0]),
        "pp": _pp(inp), "lbrow": f(np.asarray(inp["hgrn_lb"]).reshape(-1)), "consts": _consts(),
    }
    maps = []
    for b in range(ncores):
        m = dict(shared)
        m["xT"] = f(np.asarray(inp["x"][b]).T)
        m["memT"] = f(np.asarray(inp["mem"][b]).T)
        maps.append(m)
    return maps


_NC_CACHE = {}


def kernel(**inputs):
    x = np.asarray(inputs["x"])
    B, T, _ = x.shape
    if T not in _NC_CACHE:
        _NC_CACHE[T] = build(T)
    nc = _NC_CACHE[T]
    in_maps = make_in_maps(inputs, B, T)
    res = run_bass_kernel_spmd(nc, in_maps, core_ids=list(range(B)))
    out = np.stack([np.asarray(r["outT"]).T for r in res.results], axis=0)
    return np.ascontiguousarray(out.astype(np.float32))
```

```python
import numpy as np
from contextlib import ExitStack

import concourse.bass as bass
import concourse.mybir as mybir
from concourse.bass_utils import run_bass_kernel_spmd

F32 = mybir.dt.float32
BF16 = mybir.dt.bfloat16
AF = mybir.ActivationFunctionType
ALU = mybir.AluOpType

PE, ACT, DVE, POOL, SP = "tensor", "scalar", "vector", "gpsimd", "sync"
ENGS = (PE, ACT, DVE, POOL, SP)

D = 1024
TT = 512
MEM = 256
DFF = 2816
NIN = 2944
FM_CH = 18
VCOL = 2304
FTM = 1280
ITM = 2432
EPS = 1e-6

G_MIX_PRE, G_MIX_POST, G_X_PRE, G_X_POST, G_FFN_PRE, G_FFN_POST, G_MEM = 0, 8, 16, 24, 32, 40, 48
ONG, LB0, LB1, SNK = 56, 57, 61, 65
NPP = 73


class Prog:
    def __init__(self, sems, dma_sems):
        self.sem = sems
        self.dma_sems = list(dma_sems)
        self.stream_sem = {}
        self.stream_cnt = {}
        self.cnt = {e: 0 for e in ENGS}
        self.lists = {e: [] for e in ENGS}
        self.waited = {e: {} for e in ENGS}
        self.last_w = {}
        self.readers = {}
        self.sem_by_id = {}
        for e in ENGS:
            self.sem_by_id[id(sems[e])] = sems[e]
        self.stage = ""
        self.labels = {e: [] for e in ENGS}

    def _deps(self, reads, writes):
        deps = []
        for k in reads:
            if k in self.last_w:
                deps.append(self.last_w[k])
        for k in writes:
            if k in self.last_w:
                deps.append(self.last_w[k])
            deps.extend(self.readers.get(k, ()))
        return deps

    def _emit_waits(self, eng, deps):
        best = {}
        for (sid, val, peng) in deps:
            if peng == eng and eng in (PE, SP):
                continue
            if self.waited[eng].get(sid, 0) >= val:
                continue
            if best.get(sid, 0) < val:
                best[sid] = val
        for sid, val in best.items():
            self.waited[eng][sid] = val
            sem = self.sem_by_id[sid]
            self.lists[eng].append(lambda e, sem=sem, val=val: e.wait_ge(sem, val))

    def _commit(self, tok, reads, writes):
        for k in reads:
            self.readers.setdefault(k, []).append(tok)
        for k in writes:
            self.last_w[k] = tok
            self.readers[k] = []

    def op(self, eng, fn, reads=(), writes=()):
        reads = list(reads)
        writes = list(writes)
        self._emit_waits(eng, self._deps(reads, writes))
        self.cnt[eng] += 1
        self.labels[eng].append(self.stage)
        sem = self.sem[eng]
        tok = (id(sem), self.cnt[eng], eng)
        self.lists[eng].append(lambda e, fn=fn, sem=sem: fn(e).then_inc(sem, 1))
        self._commit(tok, reads, writes)

    def dma(self, qeng, stream, fn, reads=(), writes=()):
        reads = list(reads)
        writes = list(writes)
        if stream not in self.stream_sem:
            self.stream_sem[stream] = self.dma_sems.pop()
            self.stream_cnt[stream] = 0
            s = self.stream_sem[stream]
            self.sem_by_id[id(s)] = s
        sem = self.stream_sem[stream]
        self._emit_waits(qeng, self._deps(reads, writes))
        self.stream_cnt[stream] += 16
        tok = (id(sem), self.stream_cnt[stream], None)
        self.lists[qeng].append(lambda e, fn=fn, sem=sem: fn(e).then_inc(sem, 16))
        self._commit(tok, reads, writes)

    def barrier(self):
        toks = [(id(self.sem[e]), self.cnt[e], e) for e in ENGS if self.cnt[e] > 0 and e != SP]
        toks += [(id(s), self.stream_cnt[n], None) for n, s in self.stream_sem.items()]
        for e in ENGS:
            self._emit_waits(e, [t for t in toks if t[2] != e or e not in (PE, SP)])
        self.last_w = {}
        self.readers = {}

    def wait_keys(self, eng, keys):
        deps = [self.last_w[k] for k in keys if k in self.last_w]
        self._emit_waits(eng, deps)

    def replay(self, block):
        lists = self.lists

        @block.tensor
        def _(e):
            for f in lists[PE]:
                f(e)

        @block.scalar
        def _(e):
            for f in lists[ACT]:
                f(e)

        @block.vector
        def _(e):
            for f in lists[DVE]:
                f(e)

        @block.gpsimd
        def _(e):
            for f in lists[POOL]:
                f(e)

        @block.sync
        def _(e):
            for f in lists[SP]:
                f(e)


def mm(out, lhsT, rhs, start, stop):
    return lambda e: e.matmul(out, lhsT, rhs, start=start, stop=stop)


def group(fns):
    def f(e):
        last = None
        for g in fns:
            last = g(e)
        return last
    return f


def act(out, in_, func, bias=None, scale=None):
    kw = {}
    if bias is not None:
        kw["bias"] = bias
    if scale is not None:
        kw["scale"] = scale
    return lambda e: e.activation(out=out, in_=in_, func=func, **kw)


def tt(out, in0, in1, op):
    return lambda e: e.tensor_tensor(out=out, in0=in0, in1=in1, op=op)


def ts(out, in0, s1, s2, op0, op1):
    return lambda e: e.tensor_scalar(out=out, in0=in0, scalar1=s1, scalar2=s2, op0=op0, op1=op1)


def stt(out, in0, scalar, in1, op0, op1):
    return lambda e: e.scalar_tensor_tensor(out=out, in0=in0, scalar=scalar, in1=in1, op0=op0, op1=op1)


def cp(out, in_):
    return lambda e: e.tensor_copy(out=out, in_=in_)


def recip(out, in_):
    return lambda e: e.reciprocal(out=out, in_=in_)


def mset(ap, v):
    return lambda e: e.memset(ap, v)


def dma(out, in_):
    return lambda e: e.dma_start(out=out, in_=in_)


class Arena:
    def __init__(self, tensor, nbytes):
        self.t = tensor
        self.nbytes = nbytes
        self.off = 0

    def reset(self, off=0):
        self.off = off

    def alloc(self, shape, dtype):
        esz = 4 if dtype == F32 else 2
        n = int(np.prod(shape[1:]))
        nb = (n * esz + 31) // 32 * 32
        assert self.off + nb <= self.nbytes, ("SBUF arena overflow", self.off, nb, self.nbytes)
        a = self.t[:, self.off // 4:(self.off + nb) // 4]
        if dtype != F32:
            a = a.bitcast(dtype)
        a = a[:, 0:n]
        self.off += nb
        if len(shape) == 3:
            a = a.rearrange("p (a b) -> p a b", a=shape[1])
        elif len(shape) == 4:
            a = a.rearrange("p (a b c) -> p a b c", a=shape[1], b=shape[2])
        return a


class _Stop(Exception):
    pass


def build(T, debug=False, stop_at=None):
    NT = T // TT

    _prog = []

    def ckpt(name):
        if _prog:
            _prog[0].stage = name
        if stop_at is not None and name == stop_at:
            raise _Stop()
    nc = bass.Bass("TRN2", target_bir_lowering=False)
    xT = nc.dram_tensor("xT", [D, T], F32, kind="ExternalInput").ap()
    memT = nc.dram_tensor("memT", [D, MEM], F32, kind="ExternalInput").ap()
    w_in = nc.dram_tensor("w_in", [D, NIN], F32, kind="ExternalInput").ap()
    w_out = nc.dram_tensor("w_out", [D, D], F32, kind="ExternalInput").ap()
    wq = nc.dram_tensor("wq", [D, D], F32, kind="ExternalInput").ap()
    wk = nc.dram_tensor("wk", [D, D], F32, kind="ExternalInput").ap()
    wv = nc.dram_tensor("wv", [D, D], F32, kind="ExternalInput").ap()
    wo = nc.dram_tensor("wo", [D, D], F32, kind="ExternalInput").ap()
    w_gate = nc.dram_tensor("w_gate", [D, DFF], F32, kind="ExternalInput").ap()
    w_up = nc.dram_tensor("w_up", [D, DFF], F32, kind="ExternalInput").ap()
    w_down = nc.dram_tensor("w_down", [DFF, D], F32, kind="ExternalInput").ap()
    pp_d = nc.dram_tensor("pp", [128, NPP], F32, kind="ExternalInput").ap()
    lbrow_d = nc.dram_tensor("lbrow", [1024], F32, kind="ExternalInput").ap()
    consts_d = nc.dram_tensor("consts", [128, 320], F32, kind="ExternalInput").ap()
    outT = nc.dram_tensor("outT", [D, T], F32, kind="ExternalOutput").ap()
    wb = {}
    for nm, shp in (("wq", [D, D]), ("wk", [D, D]), ("wv", [D, D]), ("wo", [D, D]),
                    ("w_gate", [D, DFF]), ("w_up", [D, DFF]), ("w_down", [DFF, D])):
        wb[nm] = nc.dram_tensor(nm + "_b", shp, BF16, kind="Internal").ap()
    kind_dbg = "ExternalOutput" if debug else "Internal"
    h1T = nc.dram_tensor("h1T", [D, T], F32, kind=kind_dbg).ap()
    h2T = nc.dram_tensor("h2T", [D, T], F32, kind=kind_dbg).ap()

    def dview(ap):
        return ap.rearrange("(c p) t -> p c t", p=128)

    with ExitStack() as es:
        NBYTES = 212480
        arena_t = es.enter_context(nc.sbuf_tensor("arena", [128, NBYTES // 4], F32))
        AR = Arena(arena_t, NBYTES)
        banks = [es.enter_context(nc.psum_tensor(f"bank{i}", [128, 512], F32)) for i in range(8)]
        sems = {e: es.enter_context(nc.semaphore(f"s_{e}")) for e in ENGS}
        dsems = [es.enter_context(nc.semaphore(f"d{i}")) for i in range(40)]
        block = es.enter_context(nc.Block())
        P = Prog(sems, dsems)
        _prog.append(P)

        try:
            def bk(i):
                return banks[i], f"ps{i}"

            PPt = AR.alloc([128, NPP], F32)
            CON = AR.alloc([128, 320], F32)
            TRIINC = CON[:, 0:128]
            TRIREV = CON[:, 128:256]
            MASK = CON[:, 256:320]
            ONES = AR.alloc([128, 128], BF16)
            LBP = AR.alloc([128, 12], F32)
            ES = AR.alloc([128, 8], F32)
            RSTD = AR.alloc([128, 512], F32)
            base_off = AR.off

            P.dma(SP, "c0", dma(PPt, pp_d), writes=["PP"])
            P.dma(SP, "c1", dma(CON, consts_d), writes=["CON"])
            P.op(POOL, mset(ONES, 1.0), writes=["ONES"])
            P.op(DVE, tt(LBP[:, 0:4], PPt[:, LB1:LB1 + 4], PPt[:, LB0:LB0 + 4], ALU.subtract), reads=["PP"], writes=["LBP"])
            P.op(ACT, act(LBP[:, 0:4], LBP[:, 0:4], AF.Exp), reads=["LBP"], writes=["LBP"])
            P.op(DVE, ts(LBP[:, 0:4], LBP[:, 0:4], 1.0, None, ALU.add, ALU.bypass) if False else
                 (lambda e: e.tensor_scalar_add(out=LBP[:, 0:4], in0=LBP[:, 0:4], scalar1=1.0)), reads=["LBP"], writes=["LBP"])
            P.op(DVE, recip(LBP[:, 0:4], LBP[:, 0:4]), reads=["LBP"], writes=["LBP"])
            P.op(DVE, ts(LBP[:, 4:8], LBP[:, 0:4], -1.0, 1.0, ALU.mult, ALU.add), reads=["LBP"], writes=["LBP"])
            P.op(DVE, ts(LBP[:, 8:12], LBP[:, 0:4], 1.0, -1.0, ALU.mult, ALU.add), reads=["LBP"], writes=["LBP"])
            P.op(ACT, act(ES, PPt[:, SNK:SNK + 8], AF.Exp), reads=["PP"], writes=["ES"])

            def rstd_from_sq(SQ, nfree, dim, bank_i, out_rstd, sqkeys, rkey="RSTD"):
                b, bkey = bk(bank_i)
                nch = SQ.shape[1]
                P.op(PE, group([mm(b[:, 0:nfree], ONES, SQ[:, c, :], c == 0, c == nch - 1) for c in range(nch)]),
                     reads=list(sqkeys) + ["ONES"], writes=[bkey])
                P.op(ACT, act(out_rstd, b[:, 0:nfree], AF.Ln, bias=EPS, scale=1.0 / dim), reads=[bkey], writes=[rkey])
                P.op(ACT, act(out_rstd, out_rstd, AF.Exp, scale=-0.5), reads=[rkey], writes=[rkey])

            def pre_norm(X, UT, gcol, bank_i):
                for c in range(8):
                    eng = ACT if c % 2 == 0 else POOL
                    if eng == ACT:
                        P.op(ACT, act(UT[:, c, :], X[:, c, :], AF.Square), reads=[("X", c)], writes=[("UT", c)])
                    else:
                        P.op(POOL, tt(UT[:, c, :], X[:, c, :], X[:, c, :], ALU.mult), reads=[("X", c)], writes=[("UT", c)])
                rstd_from_sq(UT, TT, D, bank_i, RSTD, [("UT", c) for c in range(8)])
                for c in range(8):
                    P.op(DVE, stt(UT[:, c, :], X[:, c, :], PPt[:, gcol + c:gcol + c + 1], RSTD, ALU.mult, ALU.mult),
                         reads=[("X", c), "RSTD", "PP"], writes=[("UT", c)])

            def post_norm_add(Y, ykeys, X, SQ, sqkeyname, gcol, bank_i):
                for j in range(8):
                    P.op(POOL, tt(SQ[:, j, :], Y[:, j, :], Y[:, j, :], ALU.mult), reads=[ykeys[j]], writes=[(sqkeyname, j)])
                rstd_from_sq(SQ, TT, D, bank_i, RSTD, [(sqkeyname, j) for j in range(8)])
                for j in range(8):
                    P.op(DVE, stt(Y[:, j, :], Y[:, j, :], PPt[:, gcol + j:gcol + j + 1], RSTD, ALU.mult, ALU.mult),
                         reads=[ykeys[j], "RSTD", "PP"], writes=[ykeys[j]])
                    P.op(POOL, tt(X[:, j, :], X[:, j, :], Y[:, j, :], ALU.add), reads=[ykeys[j], ("X", j)], writes=[("X", j)])

            def load_w(dst, src, kch, stream, key):
                sv = src.rearrange("(k p) n -> p k n", p=128)
                for k in range(kch):
                    P.dma(POOL, stream, dma(dst[:, k, :], sv[:, k, :]), writes=[key])

            def cvt_w(nm, src, rows):
                for k in range(rows // 128):
                    P.dma(POOL, "cvt", dma(wb[nm][k * 128:(k + 1) * 128, :], src[k * 128:(k + 1) * 128, :]), writes=["b_" + nm])

            def load_wb(dst, nm, kch, stream, key):
                sv = wb[nm].rearrange("(k p) n -> p k n", p=128)
                for k0 in range(0, kch, 8):
                    k1 = min(kch, k0 + 8)
                    P.dma(SP, stream, dma(dst[:, k0:k1, :], sv[:, k0:k1, :]), reads=["b_" + nm], writes=[key])

            AR.reset(base_off)
            WIN = AR.alloc([128, 8, NIN], BF16)
            WOUT = AR.alloc([128, 8, D], BF16)
            X = AR.alloc([128, 8, TT], F32)
            UT = AR.alloc([128, 8, TT], BF16)
            QA = AR.alloc([128, 4, TT], BF16)
            KK = AR.alloc([128, 2, 128 + TT], BF16)
            VAUG = AR.alloc([128, 10, 2, 128], BF16)
            QFKF = AR.alloc([128, 8, TT], F32)
            XR = AR.alloc([128, 8, TT], F32)
            SG = XR[:, 0:4, :]
            BT = XR[:, 4:8, :]
            SQ2 = AR.alloc([128, 8, 512], BF16)
            VT = SQ2[:, 0:4, :]
            KHAT = SQ2[:, 4:8, :]
            SGT = AR.alloc([128, 512], F32)
            LOGF = AR.alloc([128, 512], F32)
            KFT = AR.alloc([128, 512], F32)
            EXRB = SGT
            TMB = [(SGT, LOGF, KFT), tuple(AR.alloc([128, 512], F32) for _ in range(3))]
            E1 = [SGT, LOGF]
            G = KFT
            GINV = G
            QT = [AR.alloc([128, TT], BF16) for _ in range(4)]
            KTL = [AR.alloc([128, TT], BF16) for _ in range(4)]
            QH = [AR.alloc([128, TT], BF16) for _ in range(4)]
            EM = AR.alloc([128, 4, 8], F32)
            DEC = AR.alloc([128, 4, 8], F32)
            EMP = AR.alloc([128, 4, 8], F32)
            AT = AR.alloc([128, 4, 4, 64], BF16)
            S32 = AR.alloc([128, 4, 128], F32)
            SBF = AR.alloc([128, 2, 4, 128], BF16)
            OSB = [AR.alloc([128, TT], F32) for _ in range(2)]
            OSQ = AR.alloc([128, 2, TT], BF16)
            RSO = [AR.alloc([128, TT], F32) for _ in range(2)]
            CAT = AR.alloc([128, 8, TT], BF16)
            PT = [AR.alloc([128, 384], BF16) for _ in range(4)]
            RR = [AR.alloc([128, 512], F32) for _ in range(2)]
            ESK = AR.alloc([128, 2, 512], F32)
            LBB = AR.alloc([128, 3, 512], F32)
            print("phase A sbuf bytes", AR.off)
            QF = QFKF[:, 0:4]
            KF = QFKF[:, 4:8]
            Y = QFKF

            WBLK = [(0, 768), (768, 1792), (1792, 2304), (2304, NIN)]
            w_in_v = w_in.rearrange("(k p) n -> p k n", p=128)
            for bi, (c0, c1) in enumerate(WBLK):
                P.dma(POOL, "win%d" % bi, dma(WIN[:, :, c0:c1], w_in_v[:, :, c0:c1]), writes=[("WIN", bi)])

            def wkey(col):
                for bi, (c0, c1) in enumerate(WBLK):
                    if c0 <= col < c1:
                        return ("WIN", bi)
            load_w(WOUT, w_out, 8, "wout", "WOUT")
            cvt_jobs = []
            for nm, src, rows in (("wk", wk, D), ("wv", wv, D), ("wq", wq, D), ("wo", wo, D),
                                  ("w_gate", w_gate, D), ("w_up", w_up, D), ("w_down", w_down, DFF)):
                for k in range(rows // 128):
                    cvt_jobs.append((nm, src, k))

            def emit_cvt(n):
                for _ in range(n):
                    if cvt_jobs:
                        nm, src, k = cvt_jobs.pop(0)
                        P.dma(POOL, "cvt", dma(wb[nm][k * 128:(k + 1) * 128, :], src[k * 128:(k + 1) * 128, :]), writes=["b_" + nm])

            P.op(POOL, mset(VAUG, 1.0), writes=["VAUG"])
            P.op(POOL, mset(S32, 0.0), writes=[("S32", h) for h in range(4)])
            P.op(POOL, mset(SBF, 0.0), writes=[("SBF", h, p) for h in range(4) for p in range(2)])
            P.op(POOL, mset(KK, 0.0), writes=["KK"])
            P.dma(SP, "c2", dma(LBB[:, 0:2, :].rearrange("p a b -> p (a b)"), lbrow_d.partition_broadcast(128)), writes=["LBB"])
            P.op(DVE, tt(LBB[:, 2, :], LBB[:, 1, :], LBB[:, 0, :], ALU.subtract), reads=["LBB"], writes=["LBB"])
            P.op(ACT, act(LBB[:, 2, :], LBB[:, 2, :], AF.Exp), reads=["LBB"], writes=["LBB"])
            P.op(DVE, lambda e: e.tensor_scalar_add(out=LBB[:, 2, :], in0=LBB[:, 2, :], scalar1=1.0), reads=["LBB"], writes=["LBB"])
            P.op(DVE, recip(LBB[:, 0, :], LBB[:, 2, :]), reads=["LBB"], writes=["LBB"])
            P.op(DVE, ts(LBB[:, 1, :], LBB[:, 0, :], -1.0, 1.0, ALU.mult, ALU.add), reads=["LBB"], writes=["LBB"])
            LB_BC = LBB[:, 0, :]
            OML_BC = LBB[:, 1, :]
            for g in range(2):
                for cc in range(2):
                    P.op(DVE, cp(ESK[:, g, cc * 256:(cc + 1) * 256].rearrange("p (h q) -> p h q", h=4),
                                 ES[:, 4 * g:4 * g + 4].unsqueeze(2).to_broadcast([128, 4, 64])), reads=["ES"], writes=["ESK"])

            ckpt('setup')
            xv = dview(xT)
            h1v = dview(h1T)
            h2v = dview(h2T)
            ov = dview(outT)

            xrk = [("SG", j) for j in range(4)] + [("BT", j) for j in range(4)]
            sq2k = [("VT", j) for j in range(4)] + [("KHAT", j) for j in range(4)]
            P.dma(SP, "x", dma(X, xv[:, :, 0:TT]), writes=[("X", c) for c in range(8)])
            pre_norm(X, UT, G_MIX_PRE, 7)
            for t in range(NT):
                tok = slice(t * TT, (t + 1) * TT)
                if t + 1 < NT:
                    P.dma(SP, "x", dma(X, xv[:, :, (t + 1) * TT:(t + 2) * TT]), writes=[("X", c) for c in range(8)])
                ckpt('prenorm')
                utk = [("UT", k) for k in range(8)]
                for j in range(FM_CH):
                    ckpt(f'fm{j}')
                    b, bkey = bk(j % 2)
                    P.op(PE, group([mm(b[:, :], WIN[:, k, j * 128:(j + 1) * 128], UT[:, k, :], k == 0, k == 7) for k in range(8)]),
                         reads=utk + [wkey(j * 128)], writes=[bkey])
                    if j < 4:
                        P.op(ACT if j % 2 == 0 else DVE, (act(QA[:, j, :], b[:, :], AF.Copy) if j % 2 == 0 else cp(QA[:, j, :], b[:, :])),
                             reads=[bkey], writes=["QA"])
                    elif j < 6:
                        P.op(DVE, cp(KK[:, j - 4, 128:128 + TT], b[:, :]), reads=[bkey], writes=["KK"])
                    elif j < 10:
                        P.op(ACT, act(QF[:, j - 6, :], b[:, :], AF.Silu), reads=[bkey], writes=[("QFKF", j - 6)])
                    elif j < 14:
                        h = j - 10
                        P.op(ACT, act(KF[:, h, :], b[:, :], AF.Sigmoid), reads=[bkey], writes=[("QFKF", 4 + h)])
                        P.op(DVE, ts(KF[:, h, :], KF[:, h, :], LBP[:, 8 + h:9 + h], LBP[:, 4 + h:5 + h], ALU.mult, ALU.add),
                             reads=[("QFKF", 4 + h), "LBP"], writes=[("QFKF", 4 + h)])
                    else:
                        P.op(ACT, act(SG[:, j - 14, :], b[:, :], AF.Silu), reads=[bkey], writes=[("SG", j - 14)])
                ckpt('fm')
                for half in range(2):
                    b, bkey = bk(2 + half)
                    fns = []
                    for cc in range(4):
                        ch = half * 4 + cc
                        for k in range(8):
                            fns.append(mm(b[0:64, cc * 128:(cc + 1) * 128], UT[:, k, ch * 64:(ch + 1) * 64],
                                          WIN[:, k, VCOL:VCOL + 128], k == 0, k == 7))
                    P.op(PE, group(fns), reads=utk + [wkey(VCOL)], writes=[bkey])
                    P.op(DVE, cp(VAUG[0:64, 2 + half * 4:6 + half * 4, :, 0:64],
                                 b[0:64, :].rearrange("p (c g d) -> p c g d", c=4, g=2)), reads=[bkey], writes=["VAUG"])
                ckpt('tmv')
                tm_steps = []

                def tm_block(s):
                    tsl = slice(s * 128, (s + 1) * 128)
                    sgt, logf, kft = TMB[s % 2]
                    k0, k1, k2 = ("T0", "T1", "T2") if s % 2 == 0 else ("T0b", "T1b", "T2b")
                    bF, kF = bk(6)
                    bI, kI = bk(7)
                    bB, kB = bk(6)
                    bR, kR = bk(7)

                    def stA():
                        P.op(PE, group([mm(bF[:, :], UT[:, k, tsl], WIN[:, k, FTM:FTM + 512], k == 0, k == 7) for k in range(8)]),
                             reads=utk + [wkey(FTM)], writes=[kF])
                        P.op(PE, group([mm(bI[:, :], UT[:, k, tsl], WIN[:, k, ITM:ITM + 512], k == 0, k == 7) for k in range(8)]),
                             reads=utk + [wkey(ITM)], writes=[kI])

                    def stB():
                        P.op(ACT, act(sgt, bF[:, :], AF.Sigmoid), reads=[kF], writes=[k0])
                        P.op(DVE, cp(VT[:, s, :], bI[:, :]), reads=[kI], writes=[("VT", s)])

                    def stC():
                        P.op(DVE, tt(sgt, sgt, OML_BC, ALU.mult), reads=[k0, "LBB"], writes=[k0])
                        P.op(DVE, tt(sgt, sgt, LB_BC, ALU.add), reads=[k0, "LBB"], writes=[k0])

                    def stD():
                        P.op(ACT, act(logf, sgt, AF.Ln), reads=[k0], writes=[k1])
                        P.op(POOL, ts(kft, sgt, -1.0, 1.0, ALU.mult, ALU.add), reads=[k0], writes=[k2])

                    def stE():
                        P.op(PE, group([mm(bB[:, h * 128:(h + 1) * 128], logf[:, h * 128:(h + 1) * 128], TRIINC, True, True) for h in range(4)]),
                             reads=[k1, "CON"], writes=[kB])

                    def stF():
                        P.op(DVE, cp(BT[:, :, tsl], bB[:, :].rearrange("p (h t) -> p h t", h=4)), reads=[kB], writes=[("BT", h) for h in range(4)])
                        P.op(PE, mm(bR[:, :], TRIREV, logf, True, True), reads=[k1, "CON"], writes=[kR])

                    def stG():
                        P.op(ACT, act(sgt, bR[:, :], AF.Exp), reads=[kR], writes=[k0])
                        P.op(POOL, tt(KHAT[:, s, :], kft, sgt, ALU.mult), reads=[k2, k0], writes=[("KHAT", s)])
                    return [stA, stB, stC, stD, stE, stF, stG]

                for s0 in (0, 2):
                    a_, b_ = tm_block(s0), tm_block(s0 + 1)
                    tm_steps.extend([a_[0], a_[1], b_[0], b_[1], a_[2], b_[2], a_[3], b_[3],
                                     a_[4], a_[5], a_[6], b_[4], b_[5], b_[6]])

                ckpt('tmh')
                units = []
                for g in range(2):
                    for c2 in range(4):
                        for cc in range(2):
                            for i2 in range(2):
                                units.append((g, c2, cc, i2))

                def emit_S(u):
                    g, c2, cc, i2 = units[u]
                    c = c2 * 2 + cc
                    gc = t * 8 + c
                    parts = [p for p in range(3) if gc - 2 + p >= 0]
                    bS, kS = bk(i2 * 2 + (u // 2) % 2)
                    pt = PT[u % 4]
                    ptk = ("PT", u % 4)
                    rows = slice(i2 * 64, i2 * 64 + 64)
                    fns = []
                    for p in parts:
                        kc = c + p
                        for pr in range(2):
                            fns.append(mm(bS[0:64, (p * 2 + pr) * 64:(p * 2 + pr + 1) * 64],
                                          KK[rows, g, kc * 64:(kc + 1) * 64],
                                          QA[rows, 2 * g + pr, c * 64:(c + 1) * 64], True, True))
                    P.op(PE, group(fns), reads=["KK", "QA"], writes=[kS])
                    lo = parts[0] * 128
                    P.op(ACT, act(pt[0:64, lo:384], bS[0:64, lo:384], AF.Exp, scale=0.125), reads=[kS], writes=[ptk])

                def emit_PV(u):
                    g, c2, cc, i2 = units[u]
                    c = c2 * 2 + cc
                    gc = t * 8 + c
                    parts = [p for p in range(3) if gc - 2 + p >= 0]
                    grp = g * 4 + c2
                    bO, kO = bk(4 + grp % 2)
                    pt = PT[u % 4]
                    ptk = ("PT", u % 4)
                    fns = []
                    slot = (cc * 2 + i2) * 128
                    for ii, p in enumerate(parts):
                        kc = c + p
                        fns.append(mm(bO[:, slot:slot + 128], VAUG[0:64, kc, g, :], pt[0:64, p * 128:(p + 1) * 128],
                                      ii == 0, ii == len(parts) - 1))
                    P.op(PE, group(fns), reads=["VAUG", ptk], writes=[kO])
                    if cc == 1 and i2 == 1:
                        def evac(g=g, c2=c2, grp=grp, bO=bO, kO=kO):
                            r = RR[grp % 2]
                            rk = ("RR", grp % 2)
                            P.op(DVE, tt(r[0:64, :], bO[64:128, :], ESK[64:128, g, :], ALU.add), reads=[kO, "ESK"], writes=[rk])
                            P.op(ACT, act(r[0:64, :], r[0:64, :], AF.Ln), reads=[rk], writes=[rk])
                            P.op(ACT, act(r[0:64, :], r[0:64, :], AF.Exp, scale=-1.0), reads=[rk], writes=[rk])
                            ov4 = bO[0:64, :].rearrange("p (cc i pr q) -> p i pr cc q", cc=2, pr=2, i=2)
                            rv4 = r[0:64, :].rearrange("p (cc i pr q) -> p i pr cc q", cc=2, pr=2, i=2)
                            for j2 in range(2):
                                dst = CAT[j2 * 64:(j2 + 1) * 64, 2 * g:2 * g + 2, c2 * 128:(c2 + 1) * 128].rearrange("p a (cc q) -> p a cc q", cc=2)
                                P.op(DVE, tt(dst, ov4[:, j2, :, :, :], rv4[:, j2, :, :, :], ALU.mult), reads=[kO, rk],
                                     writes=[("CAT", 2 * g), ("CAT", 2 * g + 1)])
                        pending.append((u + 2, evac))

                PD = 3
                pending = []
                for u in range(min(PD, len(units))):
                    emit_S(u)
                for u in range(len(units)):
                    if u + PD < len(units):
                        emit_S(u + PD)
                    if tm_steps:
                        tm_steps.pop(0)()
                    emit_PV(u)
                    while pending and pending[0][0] <= u:
                        pending.pop(0)[1]()
                while tm_steps:
                    tm_steps.pop(0)()
                while pending:
                    pending.pop(0)[1]()
                ckpt('swa')
                P.op(POOL, cp(VAUG[0:64, 0:2, :, 0:64], VAUG[0:64, 8:10, :, 0:64]), reads=["VAUG"], writes=["VAUG"])
                P.op(POOL, cp(KK[:, :, 0:128], KK[:, :, TT:TT + 128]), reads=["KK"], writes=["KK"])

                ckpt('carry')
                for h in range(4):
                    hb = h % 2
                    e1 = E1[hb]
                    e1k = "T%d" % hb
                    G = TMB[hb][2]
                    GINV = G
                    gk = "T2" if hb == 0 else "T2b"
                    btk = ("BT", h)
                    P.op(ACT, act(e1, BT[:, h, :], AF.Exp), reads=[btk], writes=[e1k])
                    P.op(ACT, act(EM[:, h, :], BT[:, h, :].rearrange("p (c t) -> p c t", t=64)[:, :, 31], AF.Exp, scale=-1.0),
                         reads=[btk], writes=[("EM", h)])
                    e1v = e1.rearrange("p (c t) -> p c t", t=64)
                    P.op(POOL, cp(DEC[:, h, :], e1v[:, :, 63]), reads=[e1k], writes=[("DEC", h)])
                    P.op(DVE, tt(G.rearrange("p (c t) -> p c t", t=64), e1v, EM[:, h, :].unsqueeze(2).to_broadcast([128, 8, 64]), ALU.mult),
                         reads=[e1k, ("EM", h)], writes=[gk])
                    P.op(POOL, tt(QH[h], QF[:, h, :], e1, ALU.mult), reads=[("QFKF", h), e1k], writes=[("QH", h)])
                    P.op(DVE, tt(QT[h], QF[:, h, :], G, ALU.mult), reads=[("QFKF", h), gk], writes=[("QT", h)])
                    P.op(ACT, act(GINV, BT[:, h, :], AF.Exp, scale=-1.0), reads=[btk, gk], writes=[gk])
                    P.op(ACT, act(EMP[:, h, :], BT[:, h, :].rearrange("p (c t) -> p c t", t=64)[:, :, 31], AF.Exp),
                         reads=[btk], writes=[("EMP", h)])
                    P.op(DVE, tt(GINV.rearrange("p (c t) -> p c t", t=64), GINV.rearrange("p (c t) -> p c t", t=64),
                                 EMP[:, h, :].unsqueeze(2).to_broadcast([128, 8, 64]), ALU.mult), reads=[gk, ("EMP", h)], writes=[gk])
                    P.op(POOL, tt(KTL[h], KF[:, h, :], GINV, ALU.mult), reads=[("QFKF", 4 + h), gk], writes=[("KTL", h)])
                    bA, kA = bk(h)
                    P.op(PE, group([mm(bA[0:64, c * 64:(c + 1) * 64], KTL[h][:, c * 64:(c + 1) * 64], QT[h][:, c * 64:(c + 1) * 64], True, True)
                                    for c in range(8)]), reads=[("KTL", h), ("QT", h)], writes=[kA])
                    bAv = bA[0:64, :].rearrange("p (s r q) -> p s r q", s=4, r=2)
                    for par in range(2):
                        P.op(DVE, tt(AT[par * 64:(par + 1) * 64, h, :, :], bAv[:, :, par, :],
                                     MASK[0:64, :].unsqueeze(1).to_broadcast([64, 4, 64]), ALU.mult),
                             reads=[kA, "CON"], writes=[("AT", h)])

                def emit_kv(c):
                    b, bkey = bk(4 + c % 2)
                    rows = slice((c % 2) * 64, (c % 2) * 64 + 64)
                    P.op(PE, group([mm(b[:, h * 128:(h + 1) * 128], KHAT[rows, c // 2, h * 128:(h + 1) * 128],
                                       VT[rows, c // 2, h * 128:(h + 1) * 128], True, True) for h in range(4)]),
                         reads=[("KHAT", c // 2), ("VT", c // 2)], writes=[bkey])

                emit_kv(0)
                for c in range(8):
                    if c + 1 < 8:
                        emit_kv(c + 1)
                    rows = slice((c % 2) * 64, (c % 2) * 64 + 64)
                    gc = t * 8 + c
                    cur = gc % 2
                    bkv, kkv = bk(4 + c % 2)
                    for h in range(4):
                        bO, kO = bk(h)
                        P.op(PE, group([mm(bO[:, c * 64:(c + 1) * 64], VT[rows, c // 2, h * 128:(h + 1) * 128], AT[rows, h, c // 2, :], True, False),
                                        mm(bO[:, c * 64:(c + 1) * 64], SBF[:, cur, h, :], QH[h][:, c * 64:(c + 1) * 64], False, True)]),
                             reads=[("VT", c // 2), ("AT", h), ("SBF", h, cur), ("QH", h)], writes=[kO])
                        P.op(DVE, stt(S32[:, h, :], S32[:, h, :], DEC[:, h, c:c + 1], bkv[:, h * 128:(h + 1) * 128], ALU.mult, ALU.add),
                             reads=[("S32", h), ("DEC", h), kkv], writes=[("S32", h)])
                        if h % 2 == 0:
                            P.op(ACT, act(SBF[:, 1 - cur, h, :], S32[:, h, :], AF.Copy), reads=[("S32", h)], writes=[("SBF", h, 1 - cur)])
                        else:
                            P.op(POOL, cp(SBF[:, 1 - cur, h, :], S32[:, h, :]), reads=[("S32", h)], writes=[("SBF", h, 1 - cur)])
                t1b_bf = TMB[1][1].bitcast(BF16)
                osb4 = [(OSB[0], ("OSB", 0)), (OSB[1], ("OSB", 1)), (TMB[0][0], "T0"), (TMB[0][1], "T1")]
                rso4 = [(RSO[0], ("RSO", 0)), (RSO[1], ("RSO", 1)), (TMB[0][2], "T2"), (TMB[1][0], "T0b")]
                osq4 = [(OSQ[:, 0:1, :], ("OSQ", 0)), (OSQ[:, 1:2, :], ("OSQ", 1)),
                        (t1b_bf[:, 0:512].rearrange("p (a b) -> p a b", a=1), "T1b"),
                        (t1b_bf[:, 512:1024].rearrange("p (a b) -> p a b", a=1), "T1b")]
                for h in range(4):
                    bO, kO = bk(h)
                    P.op(ACT, act(osb4[h][0], bO[:, :], AF.Copy, scale=128.0 ** -0.5), reads=[kO], writes=[osb4[h][1]])
                for h in range(4):
                    P.op(POOL, tt(osq4[h][0][:, 0, :], osb4[h][0], osb4[h][0], ALU.mult), reads=[osb4[h][1]], writes=[osq4[h][1]])
                for h in range(4):
                    b, bkey = bk(4 + h)
                    P.op(PE, mm(b[:, :], ONES, osq4[h][0][:, 0, :], True, True), reads=[osq4[h][1], "ONES"], writes=[bkey])
                for h in range(4):
                    b, bkey = bk(4 + h)
                    P.op(ACT, act(rso4[h][0], b[:, :], AF.Ln, bias=EPS, scale=1.0 / 128), reads=[bkey], writes=[rso4[h][1]])
                for h in range(4):
                    P.op(ACT, act(rso4[h][0], rso4[h][0], AF.Exp, scale=-0.5), reads=[rso4[h][1]], writes=[rso4[h][1]])
                for h in range(4):
                    P.op(DVE, stt(osb4[h][0], osb4[h][0], PPt[:, ONG:ONG + 1], rso4[h][0], ALU.mult, ALU.mult),
                         reads=[osb4[h][1], rso4[h][1], "PP"], writes=[osb4[h][1]])
                for h in range(4):
                    P.op(POOL, tt(CAT[:, 4 + h, :], osb4[h][0], SG[:, h, :], ALU.mult), reads=[osb4[h][1], ("SG", h)], writes=[("CAT", 4 + h)])

                ckpt('hgrn')
                P.dma(SP, "xr", dma(XR, xv[:, :, tok]), writes=xrk)
                catk = [("CAT", k) for k in range(8)]
                for j in range(8):
                    b, bkey = bk(j % 2)
                    P.op(PE, group([mm(b[:, :], WOUT[:, k, j * 128:(j + 1) * 128], CAT[:, k, :], k == 0, k == 7) for k in range(8)]),
                         reads=catk + ["WOUT"], writes=[bkey])
                    P.op(ACT, act(Y[:, j, :], b[:, :], AF.Copy), reads=[bkey], writes=[("QFKF", j)])
                    P.op(POOL, tt(SQ2[:, j, :], Y[:, j, :], Y[:, j, :], ALU.mult), reads=[("QFKF", j)], writes=[sq2k[j]])
                if t + 1 < NT:
                    pre_norm(X, UT, G_MIX_PRE, 7)
                rstd_from_sq(SQ2, TT, D, 6, RSO[0], sq2k, rkey=("RSO", 0))
                for j in range(8):
                    P.op(DVE, stt(Y[:, j, :], Y[:, j, :], PPt[:, G_MIX_POST + j:G_MIX_POST + j + 1], RSO[0], ALU.mult, ALU.mult),
                         reads=[("QFKF", j), ("RSO", 0), "PP"], writes=[("QFKF", j)])
                    P.op(POOL, tt(XR[:, j, :], XR[:, j, :], Y[:, j, :], ALU.add), reads=[("QFKF", j), xrk[j]], writes=[xrk[j]])
                P.dma(SP, "h1", dma(h1v[:, :, tok], XR), reads=xrk, writes=[("h1", t)])
                emit_cvt(10)

            emit_cvt(len(cvt_jobs))
            ckpt('A')
            TB = 256
            NTB = T // TB

            def rstd2(SQ, sqkeys, dim, bank_i, out_rstd, rkey, nfree):
                b, bkey = bk(bank_i)
                nch = SQ.shape[1]
                P.op(PE, group([mm(b[:, 0:nfree], ONES, SQ[:, c, :], c == 0, c == nch - 1) for c in range(nch)]),
                     reads=list(sqkeys) + ["ONES"], writes=[bkey])
                P.op(ACT, act(out_rstd, b[:, 0:nfree], AF.Ln, bias=EPS, scale=1.0 / dim), reads=[bkey], writes=[rkey])
                P.op(ACT, act(out_rstd, out_rstd, AF.Exp, scale=-0.5), reads=[rkey], writes=[rkey])

            def pre_norm2(Xb, xk, Ub, uk, gcol, bank_i, R, rkey):
                for c in range(8):
                    if c % 2 == 0:
                        P.op(ACT, act(Ub[:, c, :], Xb[:, c, :], AF.Square), reads=[(xk, c)], writes=[(uk, c)])
                    else:
                        P.op(POOL, tt(Ub[:, c, :], Xb[:, c, :], Xb[:, c, :], ALU.mult), reads=[(xk, c)], writes=[(uk, c)])
                rstd2(Ub, [(uk, c) for c in range(8)], D, bank_i, R, rkey, TB)
                for c in range(8):
                    P.op(DVE, stt(Ub[:, c, :], Xb[:, c, :], PPt[:, gcol + c:gcol + c + 1], R, ALU.mult, ALU.mult),
                         reads=[(xk, c), rkey, "PP"], writes=[(uk, c)])

            def post_norm2(Yb, yk, Xb, xk, SQb, sk, gcol, bank_i, R, rkey):
                rstd2(SQb, [(sk, j) for j in range(8)], D, bank_i, R, rkey, TB)
                for j in range(8):
                    P.op(DVE, stt(Yb[:, j, :], Yb[:, j, :], PPt[:, gcol + j:gcol + j + 1], R, ALU.mult, ALU.mult),
                         reads=[(yk, j), rkey, "PP"], writes=[(yk, j)])
                    P.op(POOL, tt(Xb[:, j, :], Xb[:, j, :], Yb[:, j, :], ALU.add), reads=[(yk, j), (xk, j)], writes=[(xk, j)])

            P.barrier()
            TOP = NBYTES - 2 * 8 * DFF * 2
            AR.reset(TOP)
            WG = AR.alloc([128, 8, DFF], BF16)
            WU = AR.alloc([128, 8, DFF], BF16)
            AR.reset(TOP)
            WK = AR.alloc([128, 8, D], BF16)
            WV = AR.alloc([128, 8, D], BF16)
            AR.reset(base_off)
            WQ = AR.alloc([128, 8, D], BF16)
            WO = AR.alloc([128, 8, D], BF16)
            KTm = AR.alloc([128, 8, MEM], BF16)
            VX = AR.alloc([128, 2, D], BF16)
            MT = AR.alloc([128, 8, MEM], BF16)
            XB = [AR.alloc([128, 8, TB], F32) for _ in range(3)]
            UB = [AR.alloc([128, 8, TB], BF16) for _ in range(2)]
            MEMT = XB[2]
            MSQ = UB[0]
            memk = [(("X", 2), c) for c in range(8)]
            msqk = [(("U", 0), c) for c in range(8)]
            QXB = [AR.alloc([128, 8, TB], BF16) for _ in range(2)]
            PX = AR.alloc([128, 4, 2, TB], BF16)
            O2 = AR.alloc([128, 8, TB], BF16)
            Y = AR.alloc([128, 8, TB], F32)
            SQ = AR.alloc([128, 8, TB], BF16)
            RD = [AR.alloc([128, TB], F32) for _ in range(2)]
            RPRE = AR.alloc([128, TB], F32)
            RPOST = AR.alloc([128, TB], F32)
            print("phase B sbuf bytes", AR.off, "top", TOP)
            assert AR.off <= TOP
            def b_ld(t):
                tok = slice(t * TB, (t + 1) * TB)
                xb, xk = XB[t % 3], ("X", t % 3)
                P.dma(SP, "xb%d" % (t % 3), dma(xb, h1v[:, :, tok]), writes=[(xk, c) for c in range(8)])

            P.dma(SP, "mem", dma(MEMT, dview(memT)), writes=memk)
            load_wb(WK, "wk", 8, "wk", "WKV")
            load_wb(WV, "wv", 8, "wv", "WKV")
            b_ld(0)
            load_wb(WQ, "wq", 8, "wq", "WQ")
            if NTB > 1:
                b_ld(1)
            load_wb(WO, "wo", 8, "wo", "WO")
            for c in range(8):
                P.op(POOL, tt(MSQ[:, c, :], MEMT[:, c, :], MEMT[:, c, :], ALU.mult), reads=memk, writes=msqk)
            rstd_from_sq(MSQ, MEM, D, 7, RSTD[:, 0:MEM], msqk)
            for c in range(8):
                P.op(DVE, stt(MT[:, c, :], MEMT[:, c, :], PPt[:, G_MEM + c:G_MEM + c + 1], RSTD[:, 0:MEM], ALU.mult, ALU.mult),
                     reads=memk + ["RSTD", "PP"], writes=["MT"])
            for j in range(8):
                b, bkey = bk(j % 2)
                P.op(PE, group([mm(b[:, 0:MEM], WK[:, k, j * 128:(j + 1) * 128], MT[:, k, :], k == 0, k == 7) for k in range(8)]),
                     reads=["WKV", "MT"], writes=[bkey])
                P.op(ACT, act(KTm[:, j, :], b[:, 0:MEM], AF.Copy), reads=[bkey], writes=["KTm"])
            for mc in range(2):
                for nh in range(2):
                    b, bkey = bk(2 + nh)
                    P.op(PE, group([mm(b[:, :], MT[:, k, mc * 128:(mc + 1) * 128], WV[:, k, nh * 512:(nh + 1) * 512], k == 0, k == 7) for k in range(8)]),
                         reads=["WKV", "MT"], writes=[bkey])
                    P.op(DVE, cp(VX[:, mc, nh * 512:(nh + 1) * 512], b[:, :]), reads=[bkey], writes=["VX"])

            def b_pre(t):
                ckpt('b_g1')
                xb, xk = XB[t % 3], ("X", t % 3)
                ub, uk = UB[t % 2], ("U", t % 2)
                pre_norm2(xb, xk, ub, uk, G_X_PRE, 7, RPRE, "RPRE")

            def b_q(t):
                ckpt('b_q')
                ub, uk = UB[t % 2], ("U", t % 2)
                utk = [(uk, k) for k in range(8)]
                qx, qk = QXB[t % 2], ("QX", t % 2)
                for j in range(8):
                    b, bkey = bk(6 + j % 2)
                    P.op(PE, group([mm(b[:, 0:TB], WQ[:, k, j * 128:(j + 1) * 128], ub[:, k, :], k == 0, k == 7) for k in range(8)]),
                         reads=utk + ["WQ"], writes=[bkey])
                    if j % 2 == 0:
                        P.op(ACT, act(qx[:, j, :], b[:, 0:TB], AF.Copy), reads=[bkey], writes=[(qk, j)])
                    else:
                        P.op(DVE, cp(qx[:, j, :], b[:, 0:TB]), reads=[bkey], writes=[(qk, j)])

            def b_g2(t):
                tok = slice(t * TB, (t + 1) * TB)
                xb, xk = XB[t % 3], ("X", t % 3)
                qx, qk = QXB[t % 2], ("QX", t % 2)

                def scores(h):
                    ckpt('b_sc')
                    for mc in range(2):
                        b, bkey = bk((h % 2) * 2 + mc)
                        P.op(PE, group([mm(b[:, 0:TB], KTm[:, 2 * h + kk, mc * 128:(mc + 1) * 128], qx[:, 2 * h + kk, :], kk == 0, kk == 1) for kk in range(2)]),
                             reads=["KTm", (qk, 2 * h), (qk, 2 * h + 1)], writes=[bkey])
                        P.op(ACT, act(PX[:, h, mc, :], b[:, 0:TB], AF.Exp, scale=1.0 / 16.0), reads=[bkey], writes=[("PX", h)])

                def pv(h):
                    ckpt('b_pv')
                    bD, kD = bk(4)
                    P.op(PE, group([mm(bD[:, 0:TB], ONES, PX[:, h, mc, :], mc == 0, mc == 1) for mc in range(2)]),
                         reads=["ONES", ("PX", h)], writes=[kD])
                    rd = RD[h % 2]
                    rdk = ("RD", h % 2)
                    P.op(ACT, act(rd, bD[:, 0:TB], AF.Ln), reads=[kD], writes=[rdk])
                    P.op(ACT, act(rd, rd, AF.Exp, scale=-1.0), reads=[rdk], writes=[rdk])
                    for dd in range(2):
                        j = 2 * h + dd
                        b, bkey = bk(5 + dd)
                        P.op(PE, group([mm(b[:, 0:TB], VX[:, mc, j * 128:(j + 1) * 128], PX[:, h, mc, :], mc == 0, mc == 1) for mc in range(2)]),
                             reads=["VX", ("PX", h)], writes=[bkey])
                        P.op(DVE, tt(O2[:, j, :], b[:, 0:TB], rd, ALU.mult), reads=[bkey, rdk], writes=[("O2", j)])

                scores(0)
                for h in range(4):
                    if h + 1 < 4:
                        scores(h + 1)
                    pv(h)

            def b_wo(t):
                ckpt('b_wo')
                o2k = [("O2", k) for k in range(8)]
                for j in range(8):
                    b, bkey = bk(j % 2)
                    P.op(PE, group([mm(b[:, 0:TB], WO[:, k, j * 128:(j + 1) * 128], O2[:, k, :], k == 0, k == 7) for k in range(8)]),
                         reads=o2k + ["WO"], writes=[bkey])
                    P.op(ACT, act(Y[:, j, :], b[:, 0:TB], AF.Copy), reads=[bkey], writes=[("Y", j)])
                    P.op(POOL, tt(SQ[:, j, :], Y[:, j, :], Y[:, j, :], ALU.mult), reads=[("Y", j)], writes=[("SQ", j)])

            def b_post(t):
                ckpt('b_post')
                tok = slice(t * TB, (t + 1) * TB)
                xb, xk = XB[t % 3], ("X", t % 3)
                post_norm2(Y, "Y", xb, xk, SQ, "SQ", G_X_POST, 4, RPOST, "RPOST")
                P.dma(SP, "h2", dma(h2v[:, :, tok], xb), reads=[(xk, c) for c in range(8)], writes=[("h2", t)])

            b_pre(0)
            b_q(0)
            for t in range(NTB):
                if t + 2 < NTB:
                    b_ld(t + 2)
                if t == 1:
                    load_wb(WG, "w_gate", 8, "wg", "WKV")
                    load_wb(WU, "w_up", 8, "wu", "WKV")
                b_g2(t)
                if t + 1 < NTB:
                    b_pre(t + 1)
                b_wo(t)
                if t + 1 < NTB:
                    b_q(t + 1)
                b_post(t)

            ckpt('B')
            P.barrier()
            AR.reset(base_off)
            WD = AR.alloc([128, 22, D], BF16)
            XB = [AR.alloc([128, 8, TB], F32) for _ in range(3)]
            UB = [AR.alloc([128, 8, TB], BF16) for _ in range(2)]
            ACTT = AR.alloc([128, 22, TB], BF16)
            Y = AR.alloc([128, 8, TB], F32)
            SQ = AR.alloc([128, 8, TB], BF16)
            GS = [AR.alloc([128, TB], F32) for _ in range(2)]
            RPRE = AR.alloc([128, TB], F32)
            RPOST = AR.alloc([128, TB], F32)
            print("phase C sbuf bytes", AR.off, "top", TOP)
            assert AR.off <= TOP
            def c_ld(t):
                tok = slice(t * TB, (t + 1) * TB)
                xb, xk = XB[t % 3], ("X", t % 3)
                P.dma(SP, "xb%d" % (t % 3), dma(xb, h2v[:, :, tok]), writes=[(xk, c) for c in range(8)])

            def c_sq(t):
                xb, xk = XB[t % 3], ("X", t % 3)
                Ub, uk = UB[t % 2], ("U", t % 2)
                for c in range(8):
                    if c % 2 == 0:
                        P.op(ACT, act(Ub[:, c, :], xb[:, c, :], AF.Square), reads=[(xk, c)], writes=[(uk, c)])
                    else:
                        P.op(POOL, tt(Ub[:, c, :], xb[:, c, :], xb[:, c, :], ALU.mult), reads=[(xk, c)], writes=[(uk, c)])

            def c_g1(t):
                xb, xk = XB[t % 3], ("X", t % 3)
                Ub, uk = UB[t % 2], ("U", t % 2)
                rstd2(Ub, [(uk, c) for c in range(8)], D, 6, RPRE, "RPRE", TB)
                for c in range(8):
                    P.op(DVE, stt(Ub[:, c, :], xb[:, c, :], PPt[:, G_FFN_PRE + c:G_FFN_PRE + c + 1], RPRE, ALU.mult, ALU.mult),
                         reads=[(xk, c), "RPRE", "PP"], writes=[(uk, c)])

            def c_g2(t):
                tok = slice(t * TB, (t + 1) * TB)
                xb, xk = XB[t % 3], ("X", t % 3)
                ub, uk = UB[t % 2], ("U", t % 2)
                utk = [(uk, k) for k in range(8)]
                for j in range(22):
                    bG, kG = bk((2 * j) % 4)
                    bU, kU = bk((2 * j) % 4 + 1)
                    P.op(PE, group([mm(bG[:, 0:TB], WG[:, k, j * 128:(j + 1) * 128], ub[:, k, :], k == 0, k == 7) for k in range(8)]),
                         reads=utk, writes=[kG])
                    P.op(PE, group([mm(bU[:, 0:TB], WU[:, k, j * 128:(j + 1) * 128], ub[:, k, :], k == 0, k == 7) for k in range(8)]),
                         reads=utk, writes=[kU])
                    gs = GS[j % 2]
                    P.op(ACT, act(gs, bG[:, 0:TB], AF.Silu), reads=[kG], writes=[("GS", j % 2)])
                    P.op(DVE, tt(ACTT[:, j, :], bU[:, 0:TB], gs, ALU.mult), reads=[kU, ("GS", j % 2)], writes=[("ACTT", j)])
                    if j == 15 and t + 1 < NTB:
                        c_sq(t + 1)

            def c_g2b(t):
                tok = slice(t * TB, (t + 1) * TB)
                xb, xk = XB[t % 3], ("X", t % 3)
                ak = [("ACTT", j) for j in range(22)]
                for jo in range(8):
                    b, bkey = bk(4 + jo % 2)
                    P.op(PE, group([mm(b[:, 0:TB], WD[:, k, jo * 128:(jo + 1) * 128], ACTT[:, k, :], k == 0, k == 21) for k in range(22)]),
                         reads=ak + ["WD"], writes=[bkey])
                    P.op(ACT, act(Y[:, jo, :], b[:, 0:TB], AF.Copy), reads=[bkey], writes=[("Y", jo)])
                    P.op(POOL, tt(SQ[:, jo, :], Y[:, jo, :], Y[:, jo, :], ALU.mult), reads=[("Y", jo)], writes=[("SQ", jo)])
                post_norm2(Y, "Y", xb, xk, SQ, "SQ", G_FFN_POST, 7, RPOST, "RPOST")
                P.dma(SP, "out", dma(ov[:, :, tok], xb), reads=[(xk, c) for c in range(8)], writes=[("out", t)])

            c_ld(0)
            if NTB > 1:
                c_ld(1)
            load_wb(WD, "w_down", 22, "wd", "WD")
            c_sq(0)
            c_g1(0)
            for t in range(NTB):
                if t + 2 < NTB:
                    c_ld(t + 2)
                c_g2(t)
                if t + 1 < NTB:
                    c_g1(t + 1)
                c_g2b(t)
        except _Stop:
            pass
        P.barrier()
        P.wait_keys(SP, [("out", t) for t in range(T // 256)])
        P.replay(block)
        print("instr counts", {e: len(P.lists[e]) for e in ENGS})
        build.last_labels = P.labels
    return nc


def _consts():
    c = np.zeros((128, 320), np.float32)
    s = np.arange(128)[:, None]
    t = np.arange(128)[None, :]
    same = (s // 64) == (t // 64)
    c[:, 0:128] = (same & (s <= t)).astype(np.float32)
    c[:, 128:256] = (same & (s > t)).astype(np.float32)
    k = np.arange(64)[:, None]
    q = np.arange(64)[None, :]
    m = (k <= q).astype(np.float32)
    c[0:64, 256:320] = m
    c[64:128, 256:320] = m
    return c


def _win_perm():
    cols = list(range(0, 512))
    cols += list(range(512, 576)) * 2 + list(range(576, 640)) * 2
    cols += list(range(768, 1280)) + list(range(1280, 1792)) + list(range(2304, 2816))
    cols += list(range(640, 768))
    cols += list(range(1792, 2304))
    assert len(cols) == NIN
    return np.array(cols)


def _pp(inp):
    pp = np.zeros((128, NPP), np.float32)

    def pc(v):
        return np.ascontiguousarray(np.asarray(v, np.float32).reshape(8, 128).T)
    pp[:, G_MIX_PRE:G_MIX_PRE + 8] = pc(inp["g_mix_pre"][0])
    pp[:, G_MIX_POST:G_MIX_POST + 8] = pc(inp["g_mix_post"][0])
    pp[:, G_X_PRE:G_X_PRE + 8] = pc(inp["g_x_pre"][0])
    pp[:, G_X_POST:G_X_POST + 8] = pc(inp["g_x_post"][0])
    pp[:, G_FFN_PRE:G_FFN_PRE + 8] = pc(inp["g_ffn_pre"][0])
    pp[:, G_FFN_POST:G_FFN_POST + 8] = pc(inp["g_ffn_post"][0])
    pp[:, G_MEM:G_MEM + 8] = pc(inp["g_mem"][0])
    pp[:, ONG] = np.asarray(inp["hgrn_onorm"][0], np.float32)
    pp[:, LB0:LB0 + 4] = np.asarray(inp["hgrn_lb"][0], np.float32).reshape(4, 128).T
    pp[:, LB1:LB1 + 4] = np.asarray(inp["hgrn_lb"][1], np.float32).reshape(4, 128).T
    sk = np.asarray(inp["sinks"][0], np.float32)
    order = [4 * g + 2 * pr + i2 for g in range(2) for i2 in range(2) for pr in range(2)]
    pp[:, SNK:SNK + 8] = sk[order][None, :]
    return pp


def make_in_maps(inp, ncores, T):
    f = lambda a: np.ascontiguousarray(np.asarray(a, np.float32))
    shared = {
        "w_in": f(np.asarray(inp["w_in"][0])[:, _win_perm()]),
        "w_out": f(inp["w_out"][0]), "wq": f(inp["wq_x"][0]), "wk": f(inp["wk_x"][0]),
        "wv": f(inp["wv_x"][0]), "wo": f(inp["wo_x"][0]),
        "w_gate": f(inp["w_gate"][0]), "w_up": f(inp["w_up"][0]), "w_down": f(inp["w_down"][0]),
        "pp": _pp(inp), "lbrow": f(np.asarray(inp["hgrn_lb"]).reshape(-1)), "consts": _consts(),
    }
    maps = []
    for b in range(ncores):
        m = dict(shared)
        m["xT"] = f(np.asarray(inp["x"][b]).T)
        m["memT"] = f(np.asarray(inp["mem"][b]).T)
        maps.append(m)
    return maps


_NC_CACHE = {}


def kernel(**inputs):
    x = np.asarray(inputs["x"])
    B, T, _ = x.shape
    if T not in _NC_CACHE:
        _NC_CACHE[T] = build(T)
    nc = _NC_CACHE[T]
    in_maps = make_in_maps(inputs, B, T)
    res = run_bass_kernel_spmd(nc, in_maps, core_ids=list(range(B)))
    out = np.stack([np.asarray(r["outT"]).T for r in res.results], axis=0)
    return np.ascontiguousarray(out.astype(np.float32))
```
